# Optimizing a Trainium2 kernel written in Bass

```python
import math
import jax
import jax.numpy as jnp
from jax import lax
import numpy as np

D_MODEL = 1024
BATCH = 4
SEQ = 4096
DEPTH = 1

CTX_LEN = 256
GRID_W = 64
NORM_EPS = 1e-6
A_GROUPS = 8
A_WIDTH = 1024
A_GW = A_WIDTH // A_GROUPS
A_CHUNK = 128
B_HEADS = 8
B_HEAD_DIM = 128
B_WIDTH = B_HEADS * B_HEAD_DIM
B_CHUNK = 64
SHORT_CONV = 3
D_FF = 2816
FFN_CONV = 3
OFF_K = 0
OFF_V = OFF_K + B_WIDTH
OFF_BA = OFF_V + B_WIDTH
OFF_Q = OFF_BA + 4 * B_HEADS
OFF_Z = OFF_Q + B_WIDTH
OFF_U = OFF_Z + B_WIDTH
OFF_AV = OFF_U + A_WIDTH
OFF_GA = OFF_AV + A_WIDTH
OFF_GB = OFF_GA + D_MODEL
IN_COLS = OFF_GB + D_MODEL
STATE_COLS = OFF_Q

kernel_name = 'hybrid_gmlp_gdn_dit_block'


def rms_norm(x, g):
    xf = x.astype(jnp.float32)
    y = xf * lax.rsqrt(jnp.mean(xf * xf, axis=-1, keepdims=True) + NORM_EPS)
    return (y * g.astype(jnp.float32)).astype(x.dtype)


def layer_norm(x, g, b):
    xf = x.astype(jnp.float32)
    mu = jnp.mean(xf, axis=-1, keepdims=True)
    xc = xf - mu
    y = xc * lax.rsqrt(jnp.mean(xc * xc, axis=-1, keepdims=True) + NORM_EPS)
    return (y * g.astype(jnp.float32) + b.astype(jnp.float32)).astype(x.dtype)


def l2norm(t):
    return t * lax.rsqrt(jnp.sum(t * t, axis=-1, keepdims=True) + NORM_EPS)


def modulate(h, shift, scale):
    return h * (1.0 + scale) + shift


def to_heads(t):
    bsz, length, _ = t.shape
    return t.reshape(bsz, length, B_HEADS, B_HEAD_DIM).transpose(0, 2, 1, 3)


def from_heads(t):
    bsz, nh, length, hd = t.shape
    return t.transpose(0, 2, 1, 3).reshape(bsz, length, nh * hd)


def dwconv(x, w, n_rows):
    bsz, length, ch = x.shape
    row_len = length // n_rows
    taps = w.shape[0]
    pad = taps // 2
    xp = jnp.pad(x.reshape(bsz, n_rows, row_len, ch), ((0, 0), (0, 0), (pad, pad), (0, 0)))
    y = xp[:, :, 0:row_len] * w[0]
    for t in range(1, taps):
        y = y + xp[:, :, t:t + row_len] * w[t]
    return y.reshape(bsz, length, ch)


def gated_delta_chunked(q, k, v, beta, g, s0):
    bsz, nh, length, _ = k.shape
    dv = v.shape[-1]
    n = length // B_CHUNK

    def chunks(t):
        return jnp.moveaxis(t.reshape(bsz, nh, n, B_CHUNK, *t.shape[3:]), 2, 0)

    k, v, beta, g = chunks(k), chunks(v), chunks(beta), chunks(g)
    gc = jnp.cumsum(g, axis=-1)
    pos = jnp.arange(B_CHUNK)
    incl = pos[:, None] >= pos[None, :]
    strict = pos[:, None] > pos[None, :]
    diff = gc[..., :, None] - gc[..., None, :]
    gamma = jnp.where(incl, jnp.exp(jnp.where(incl, diff, 0.0)), 0.0)
    kb = k * beta[..., None]
    m = jnp.where(strict, jnp.einsum('nbhid,nbhjd->nbhij', kb, k) * gamma, 0.0)
    eye = jnp.eye(B_CHUNK, dtype=k.dtype)
    t_inv = lax.linalg.triangular_solve(eye + m, jnp.broadcast_to(eye, m.shape), left_side=True, lower=True)
    w_c = t_inv @ (kb * jnp.exp(gc)[..., None])
    u_c = t_inv @ (v * beta[..., None])
    g_end = gc[..., -1]
    kd = k * jnp.exp(g_end[..., None] - gc)[..., None]

    def update(s, w_i, u_i, kd_i, ge_i):
        v_new = u_i - jnp.einsum('bhck,bhkv->bhcv', w_i, s)
        s_next = s * jnp.exp(ge_i)[..., None, None] + jnp.einsum('bhck,bhcv->bhkv', kd_i, v_new)
        return v_new, s_next

    if q is None:
        def step_state(s, xs):
            _, s_next = update(s, *xs)
            return s_next, None
        s_fin, _ = lax.scan(step_state, s0, (w_c, u_c, kd, g_end))
        return None, s_fin

    q = chunks(q)
    qk = jnp.einsum('nbhid,nbhjd->nbhij', q, k) * gamma
    qg = q * jnp.exp(gc)[..., None]

    def step(s, xs):
        w_i, u_i, kd_i, ge_i, qg_i, qk_i = xs
        v_new, s_next = update(s, w_i, u_i, kd_i, ge_i)
        o = jnp.einsum('bhck,bhkv->bhcv', qg_i, s) + jnp.einsum('bhij,bhjv->bhiv', qk_i, v_new)
        return s_next, o

    s_fin, o = lax.scan(step, s0, (w_c, u_c, kd, g_end, qg, qk))
    o = jnp.moveaxis(o, 0, 2).reshape(bsz, nh, length, dv)
    return o, s_fin


def delta_prep(p, conv_w, a_log, dt_bias, with_q):
    f32 = jnp.float32
    bsz, length = p.shape[0], p.shape[1]
    kv = jax.nn.silu(dwconv(p[..., OFF_K:OFF_BA], conv_w[:, :2 * B_WIDTH], 1)).astype(f32)
    k = l2norm(to_heads(kv[..., :B_WIDTH]))
    v = to_heads(kv[..., B_WIDTH:])
    ba = p[..., OFF_BA:OFF_Q].astype(f32).reshape(bsz, length, 4, B_HEADS).transpose(2, 0, 3, 1)
    beta = jax.nn.sigmoid(ba[:2])
    g = -jnp.exp(a_log.astype(f32))[:, None, :, None] * jax.nn.softplus(
        ba[2:] + dt_bias.astype(f32)[:, None, :, None])
    q = None
    if with_q:
        qr = jax.nn.silu(dwconv(p[..., OFF_Q:OFF_Z], conv_w[:, 2 * B_WIDTH:], 1)).astype(f32)
        q = l2norm(to_heads(qr)) * (B_HEAD_DIM ** -0.5)
    return q, k, v, beta, g


def flip_seq(t):
    return jnp.flip(t, axis=2)


def bidir_delta(q, k, v, beta, g, s0):
    o_f, s_f = gated_delta_chunked(q, k, v, beta[0], g[0], s0[0])
    o_b, s_b = gated_delta_chunked(flip_seq(q), flip_seq(k), flip_seq(v), flip_seq(beta[1]),
                                   flip_seq(g[1]), s0[1])
    return o_f + flip_seq(o_b), jnp.stack([s_f, s_b])


def bidir_delta_state(k, v, beta, g):
    bsz = k.shape[0]
    zero = jnp.zeros((bsz, B_HEADS, B_HEAD_DIM, B_HEAD_DIM), jnp.float32)
    _, s_f = gated_delta_chunked(None, k, v, beta[0], g[0], zero)
    _, s_b = gated_delta_chunked(None, flip_seq(k), flip_seq(v), flip_seq(beta[1]), flip_seq(g[1]), zero)
    return jnp.stack([s_f, s_b])


def chunk_mlp(u, v, ln_g, ln_b, w_s, b_s):
    bsz, length, _ = u.shape
    v = layer_norm(v, ln_g, ln_b)
    vc = v.reshape(bsz, length // A_CHUNK, A_CHUNK, A_GROUPS, A_GW)
    s = jnp.einsum('gij,bnjgc->bnigc', w_s, vc) + b_s.T[None, None, :, :, None]
    return u * s.reshape(bsz, length, A_WIDTH)


def token_mixers(h, s0, w_in, conv_qkv, a_log, dt_bias, onorm_g, w_proj_b,
                 a_ln_g, a_ln_b, a_ws, a_bs, w_proj_a, w_out):
    p = h @ w_in
    q, k, v, beta, g = delta_prep(p, conv_qkv, a_log, dt_bias, True)
    o, states = bidir_delta(q, k, v, beta, g, s0)
    z = to_heads(p[..., OFF_Z:OFF_U]).astype(jnp.float32)
    o = rms_norm(o, onorm_g) * jax.nn.silu(z)
    y_b = from_heads(o).astype(h.dtype) @ w_proj_b
    u = jax.nn.gelu(p[..., OFF_U:OFF_AV])
    va = jax.nn.gelu(p[..., OFF_AV:OFF_GA])
    y_a = chunk_mlp(u, va, a_ln_g, a_ln_b, a_ws, a_bs) @ w_proj_a
    g_a = jax.nn.sigmoid(p[..., OFF_GA:OFF_GB])
    g_b = jax.nn.sigmoid(p[..., OFF_GB:IN_COLS])
    return (g_a * y_a + g_b * y_b) @ w_out, states


def conv_ffn(h, n_rows, w_up, conv_w, conv_b, w_down):
    a, b = jnp.split(h @ w_up, 2, axis=-1)
    a = dwconv(a, conv_w, n_rows) + conv_b
    return (jax.nn.gelu(a) * b) @ w_down


def setup_inputs(seed: int = 0) -> dict:
    key = jax.random.key(seed)
    ks = iter(jax.random.split(key, 32))
    f32 = jnp.float32

    def nrm(shape, scale):
        return jax.random.normal(next(ks), shape, f32) * scale

    def gain(shape):
        return 1.0 + 0.02 * jax.random.normal(next(ks), shape, f32)

    x = nrm((BATCH, SEQ, D_MODEL), 1.0)
    c = nrm((BATCH, D_MODEL), 1.0)
    ctx = nrm((BATCH, CTX_LEN, D_MODEL), 1.0)
    c_ctx = nrm((D_MODEL,), 1.0)
    w_mod = nrm((DEPTH, D_MODEL, 6 * D_MODEL), 0.5 * D_MODEL ** -0.5)
    b_mod = nrm((DEPTH, 6 * D_MODEL), 0.02)
    norm1_g = gain((DEPTH, D_MODEL))
    w_in = nrm((DEPTH, D_MODEL, IN_COLS), D_MODEL ** -0.5)
    conv_qkv = nrm((DEPTH, SHORT_CONV, 3 * B_WIDTH), SHORT_CONV ** -0.5)
    a_log = jnp.log(jax.random.uniform(next(ks), (DEPTH, 2, B_HEADS), f32, 1.0, 16.0))
    dt = jnp.exp(jax.random.uniform(next(ks), (DEPTH, 2, B_HEADS), f32, math.log(1e-3), math.log(1e-1)))
    dt_bias = dt + jnp.log(-jnp.expm1(-dt))
    onorm_g = gain((DEPTH, B_HEAD_DIM))
    w_proj_b = nrm((DEPTH, B_WIDTH, D_MODEL), B_WIDTH ** -0.5)
    a_ln_g = gain((DEPTH, A_WIDTH))
    a_ln_b = nrm((DEPTH, A_WIDTH), 0.02)
    a_ws = nrm((DEPTH, A_GROUPS, A_CHUNK, A_CHUNK), A_CHUNK ** -0.5)
    a_bs = 1.0 + nrm((DEPTH, A_GROUPS, A_CHUNK), 0.02)
    w_proj_a = nrm((DEPTH, A_WIDTH, D_MODEL), A_WIDTH ** -0.5)
    w_out = nrm((DEPTH, D_MODEL, D_MODEL), D_MODEL ** -0.5)
    norm2_g = gain((DEPTH, D_MODEL))
    w_up = nrm((DEPTH, D_MODEL, 2 * D_FF), D_MODEL ** -0.5)
    ffn_conv_w = nrm((DEPTH, FFN_CONV, D_FF), FFN_CONV ** -0.5)
    ffn_conv_b = nrm((DEPTH, D_FF), 0.02)
    w_down = nrm((DEPTH, D_FF, D_MODEL), D_FF ** -0.5)
    final_g = gain((D_MODEL,))
    return {'x': x, 'c': c, 'ctx': ctx, 'c_ctx': c_ctx, 'w_mod': w_mod, 'b_mod': b_mod,
            'norm1_g': norm1_g, 'w_in': w_in, 'conv_qkv': conv_qkv, 'a_log': a_log,
            'dt_bias': dt_bias, 'onorm_g': onorm_g, 'w_proj_b': w_proj_b, 'a_ln_g': a_ln_g,
            'a_ln_b': a_ln_b, 'a_ws': a_ws, 'a_bs': a_bs, 'w_proj_a': w_proj_a, 'w_out': w_out,
            'norm2_g': norm2_g, 'w_up': w_up, 'ffn_conv_w': ffn_conv_w, 'ffn_conv_b': ffn_conv_b,
            'w_down': w_down, 'final_g': final_g}


def reference(x, c, ctx, c_ctx, w_mod, b_mod, norm1_g, w_in, conv_qkv, a_log, dt_bias, onorm_g,
              w_proj_b, a_ln_g, a_ln_b, a_ws, a_bs, w_proj_a, w_out, norm2_g, w_up, ffn_conv_w,
              ffn_conv_b, w_down, final_g):
    rows = x.shape[1] // GRID_W
    cond = jax.nn.silu(c)
    cond_ctx = jax.nn.silu(c_ctx)
    for l in range(DEPTH):
        last = l == DEPTH - 1
        mod = (cond @ w_mod[l] + b_mod[l])[:, None, :]
        mod_c = cond_ctx @ w_mod[l] + b_mod[l]
        sh1, sc1, gt1, sh2, sc2, gt2 = jnp.split(mod, 6, axis=-1)
        csh1, csc1, cgt1, csh2, csc2, cgt2 = jnp.split(mod_c, 6, axis=-1)
        mix_w = (conv_qkv[l], a_log[l], dt_bias[l], onorm_g[l], w_proj_b[l], a_ln_g[l], a_ln_b[l],
                 a_ws[l], a_bs[l], w_proj_a[l], w_out[l])
        hc = modulate(rms_norm(ctx, norm1_g[l]), csh1, csc1)
        if last:
            pc = hc @ w_in[l][:, :STATE_COLS]
            _, kc, vc, betac, gcx = delta_prep(pc, conv_qkv[l], a_log[l], dt_bias[l], False)
            s_ctx = bidir_delta_state(kc, vc, betac, gcx)
        else:
            zero = jnp.zeros((2, ctx.shape[0], B_HEADS, B_HEAD_DIM, B_HEAD_DIM), jnp.float32)
            yc, s_ctx = token_mixers(hc, zero, w_in[l], *mix_w)
        hx = modulate(rms_norm(x, norm1_g[l]), sh1, sc1)
        y, _ = token_mixers(hx, s_ctx, w_in[l], *mix_w)
        x = x + gt1 * y
        hx2 = modulate(rms_norm(x, norm2_g[l]), sh2, sc2)
        x = x + gt2 * conv_ffn(hx2, rows, w_up[l], ffn_conv_w[l], ffn_conv_b[l], w_down[l])
        if not last:
            ctx = ctx + cgt1 * yc
            hc2 = modulate(rms_norm(ctx, norm2_g[l]), csh2, csc2)
            ctx = ctx + cgt2 * conv_ffn(hc2, 1, w_up[l], ffn_conv_w[l], ffn_conv_b[l], w_down[l])
    return rms_norm(x, final_g)
```

```python
import numpy as np
import os
from contextlib import ExitStack
import concourse.bass as bass
import concourse.mybir as mybir
from concourse.bass_utils import run_bass_kernel_spmd

F32 = mybir.dt.float32
BF16 = mybir.dt.bfloat16
F32R = mybir.dt.float32r
AF = mybir.ActivationFunctionType
ALU = mybir.AluOpType

D = 1024
KC = 8
SEQ = 4096
HALF = 2048
NT = 16
CTX = 256
EPS = 1e-6
IN_COLS = 8224
OFF_K, OFF_V, OFF_BA, OFF_Q, OFF_Z, OFF_U, OFF_AV, OFF_GA, OFF_GB = 0, 1024, 2048, 2080, 3104, 4128, 5152, 6176, 7200
DFF = 2816
NFF = 22


class T:
    __slots__ = ("name", "w", "r", "x")

    def __init__(self, name, x=False):
        self.name = name
        self.w = None
        self.r = {}
        self.x = x


class Sched:
    CE = ("pe", "act", "dve", "pool")
    ALL = ("pe", "act", "dve", "pool", "sp")

    def __init__(self, nc, stack, n_dma=40):
        self.nc = nc
        self.sem = {e: stack.enter_context(nc.semaphore("s_" + e)) for e in self.CE}
        self.cnt = {e: 0 for e in self.CE}
        self.prog = {e: [] for e in self.ALL}
        self.clock = {e: {} for e in self.ALL}
        self.snap = {}
        self.dsem = [stack.enter_context(nc.semaphore("d%d" % i)) for i in range(n_dma)]
        self.dval = [0] * n_dma
        self.dpool = {"sp": list(range(0, n_dma // 2)), "pool": list(range(n_dma // 2, n_dma * 3 // 4)),
                      "act": list(range(n_dma * 3 // 4, n_dma))}
        self.dnext = {"sp": 0, "pool": 0, "act": 0}
        self.nwait = 0
        self.nop = 0

    def _semof(self, key):
        if isinstance(key, tuple):
            return self.dsem[key[1]]
        return self.sem[key]

    def need(self, eng, ev):
        key, val = ev
        ck = self.clock[eng]
        if ck.get(key, 0) >= val:
            return
        self.prog[eng].append(("wait", self._semof(key), val))
        self.nwait += 1
        sn = self.snap.get(ev)
        if sn:
            for k, v in sn.items():
                if ck.get(k, 0) < v:
                    ck[k] = v
        ck[key] = val

    def _deps(self, eng, reads, writes):
        deps = set()
        for t in reads:
            if t.w is not None:
                deps.add(t.w)
            if t.x:
                for rk, ev in t.r.items():
                    if rk != eng:
                        deps.add(ev)
        for t in writes:
            if t.w is not None:
                deps.add(t.w)
            for ev in t.r.values():
                deps.add(ev)
        if eng == "pe":
            deps = {d for d in deps if d[0] != "pe"}
        for ev in sorted(deps, key=lambda x: (str(x[0]), x[1])):
            self.need(eng, ev)

    def _mark(self, rkey, ev, reads, writes):
        for t in reads:
            t.r[rkey] = ev
        for t in writes:
            t.w = ev
            t.r = {}

    def op(self, eng, fn, reads=(), writes=()):
        self._deps(eng, reads, writes)
        self.cnt[eng] += 1
        ev = (eng, self.cnt[eng])
        self.prog[eng].append(("op", fn, self.sem[eng], 1))
        self.snap[ev] = dict(self.clock[eng])
        self._mark(eng, ev, reads, writes)
        self.nop += 1
        return ev

    def pe(self, fn, reads=(), writes=()):
        return self.op("pe", fn, reads, writes)

    def act(self, fn, reads=(), writes=()):
        return self.op("act", fn, reads, writes)

    def dve(self, fn, reads=(), writes=()):
        return self.op("dve", fn, reads, writes)

    def pool(self, fn, reads=(), writes=()):
        return self.op("pool", fn, reads, writes)

    def dma(self, q, fn, reads=(), writes=()):
        self._deps(q, reads, writes)
        lst = self.dpool[q]
        i = lst[self.dnext[q] % len(lst)]
        self.dnext[q] += 1
        key = ("d", i)
        if self.dval[i] > 0:
            self.need(q, (key, self.dval[i]))
        self.dval[i] += 16
        ev = (key, self.dval[i])
        self.prog[q].append(("op", fn, self.dsem[i], 16))
        self.snap[ev] = dict(self.clock[q])
        self._mark(key, ev, reads, writes)
        return ev

    def _all_events(self):
        evs = [(e, self.cnt[e]) for e in self.CE if self.cnt[e] > 0]
        evs += [(("d", i), v) for i, v in enumerate(self.dval) if v > 0]
        return evs

    def barrier(self):
        evs = self._all_events()
        for e in self.ALL:
            for ev in evs:
                self.need(e, ev)

    def finish(self):
        for ev in self._all_events():
            self.need("sp", ev)

    def emit(self, block):
        prog = self.prog

        def run(h, lst):
            for ent in lst:
                if ent[0] == "wait":
                    h.wait_ge(ent[1], ent[2])
                else:
                    ins = ent[1](h)
                    ins.then_inc(ent[2], ent[3])

        @block.sync
        def _(h):
            run(h, prog["sp"])

        @block.tensor
        def _(h):
            run(h, prog["pe"])

        @block.scalar
        def _(h):
            run(h, prog["act"])

        @block.vector
        def _(h):
            run(h, prog["dve"])

        @block.gpsimd
        def _(h):
            run(h, prog["pool"])


class K:
    def __init__(self, nc, stack):
        self.nc = nc
        self.st = stack
        self.S = Sched(nc, stack)
        self.dbg_outs = []

    def sb(self, name, shape, dt=F32):
        return self.st.enter_context(self.nc.sbuf_tensor(un("s_" + name), list(shape), dt))

    def dram_in(self, name, shape, dt=F32):
        return self.nc.dram_tensor(name, list(shape), dt, kind="ExternalInput").ap()

    def dram_out(self, name, shape, dt=F32):
        return self.nc.dram_tensor(name, list(shape), dt, kind="ExternalOutput").ap()


_UNIQ = [0]


def un(name):
    _UNIQ[0] += 1
    return "%s_%d" % (name, _UNIQ[0])


def build(stage="full", dbg=False):
    nc = bass.Bass("TRN2", target_bir_lowering=False)
    with ExitStack() as st:
        k = K(nc, st)
        S = k.S
        I = {}
        for name, shape in [
            ("xo", [HALF, D]), ("xt", [HALF, D]), ("cx", [CTX, D]), ("cvec", [128, KC, 2]),
            ("w_mod", [D, 6 * D]), ("b_modT", [128, 48]), ("b_modR", [1, 6 * D]),
            ("g1T", [128, KC]), ("g2T", [128, KC]), ("gfR", [128, D]),
            ("w_in", [D, IN_COLS]), ("cwT", [128, 24, 3]), ("alogR", [128, 16]), ("dtbR", [128, 16]), ("ongT", [128, 1]),
            ("w_proj_a", [D, D]), ("w_proj_b", [D, D]), ("w_out", [D, D]), ("lgR", [128, D]), ("lbrow", [1, D]),
            ("bsrow", [1, D]), ("wsT", [128, 8, 128]),
            ("w_up", [D, 2 * DFF]), ("fcwT", [128, NFF, 3]), ("fcbT", [128, NFF]), ("w_down", [DFF, D]),
        ]:
            I[name] = k.dram_in(name, shape)
        out = k.dram_out("out", [HALF, D])
        dbgo = {}
        if dbg:
            dbgo["hT"] = k.dram_out("d_hT", [128, KC, SEQ], BF16)
            dbgo["hTc"] = k.dram_out("d_hTc", [128, KC, CTX], BF16)
            dbgo["modT"] = k.dram_out("d_modT", [128, 48, 2])
            dbgo["gtR"] = k.dram_out("d_gtR", [128, 2, D])
            dbgo["kn"] = k.dram_out("d_kn", [128, SEQ + CTX])
            dbgo["vs"] = k.dram_out("d_vs", [128, SEQ + CTX])
            dbgo["qn"] = k.dram_out("d_qn", [128, HALF])
            dbgo["oT"] = k.dram_out("d_oT", [128, HALF])
            dbgo["beta"] = k.dram_out("d_beta", [128, 34, 16])
            dbgo["gc"] = k.dram_out("d_gc", [128, 34, 16])
            dbgo["Sst"] = k.dram_out("d_Sst", [128, 128])

        ident = k.sb("ident", [128, 128]); t_ident = T("ident")
        ones = k.sb("ones", [128, 128]); t_ones = T("ones")
        hT = k.sb("hT", [128, KC, HALF], BF16)
        t_hT = [T("hT%d" % i) for i in range(32)]
        hTc = k.sb("hTc", [128, KC, CTX], BF16)
        t_hTc = [T("hTc%d" % i) for i in range(2)]
        modT = k.sb("modT", [128, 48, 2]); t_modT = T("modT")
        gtR = k.sb("gtR", [128, 2, D]); t_gtR = T("gtR")
        a1T = k.sb("a1T", [128, KC, 2]); t_a1T = T("a1T")
        a2T = k.sb("a2T", [128, KC, 2]); t_a2T = T("a2T")
        g1T = k.sb("g1T", [128, KC]); t_g1T = T("g1T")
        g2T = k.sb("g2T", [128, KC]); t_g2T = T("g2T")

        X1 = k.sb("X1", [128, NT, D]); t_x1 = [T("x1_%d" % i) for i in range(NT)]
        hTt = X1[:, 0:8, :].bitcast(BF16)

        PP = [st.enter_context(nc.psum_tensor("pp%d" % i, [128, 1024], F32)) for i in range(4)]
        t_PB = [T("pb%d" % i, x=True) for i in range(8)]

        def bank(i):
            return PP[i // 2][:, (i % 2) * 512:(i % 2) * 512 + 512]

        _cast_rr = [0]

        def load_cast(stg, dst, t_dst, src, n):
            (sa, t_sa) = stg[_cast_rr[0] % len(stg)]
            eng = ("act", "dve", "pool")[_cast_rr[0] % 3]
            _cast_rr[0] += 1
            sv = sa[:, 0:n]
            if len(dst.shape) == 3:
                sv = sv.rearrange("p (a b) -> p a b", b=dst.shape[2])
            S.dma("sp", lambda e: e.dma_start(out=sv, in_=src), writes=[t_sa])
            if eng == "act":
                S.act(lambda e: e.activation(out=dst, in_=sv, func=AF.Identity), [t_sa], [t_dst])
            else:
                S.op(eng, lambda e: e.tensor_copy(out=dst, in_=sv), [t_sa], [t_dst])

        S.pool(lambda e: e.memset(ones[:], 1.0), writes=[t_ones])
        S.pool(lambda e: e.affine_select(out=ident[:], in_=ones[:], pattern=[[1, 128]], compare_op=ALU.is_equal,
                                         fill=0.0, base=0, channel_multiplier=-1),
               reads=[t_ones], writes=[t_ident])
        triU = k.sb("triU", [128, 128]); triL = k.sb("triL", [128, 128])
        strU = k.sb("strU", [128, 128]); strL = k.sb("strL", [128, 128])
        t_msk = T("masks")
        S.pool(lambda e: e.affine_select(out=triU[:], in_=ones[:], pattern=[[1, 128]], compare_op=ALU.is_ge,
                                         fill=0.0, base=0, channel_multiplier=-1), reads=[t_ones], writes=[t_msk])
        S.pool(lambda e: e.affine_select(out=strU[:], in_=ones[:], pattern=[[1, 128]], compare_op=ALU.is_gt,
                                         fill=0.0, base=0, channel_multiplier=-1), reads=[t_ones], writes=[t_msk])
        S.pool(lambda e: e.affine_select(out=triL[:], in_=ones[:], pattern=[[-1, 128]], compare_op=ALU.is_ge,
                                         fill=0.0, base=0, channel_multiplier=1), reads=[t_ones], writes=[t_msk])
        S.pool(lambda e: e.affine_select(out=strL[:], in_=ones[:], pattern=[[-1, 128]], compare_op=ALU.is_gt,
                                         fill=0.0, base=0, channel_multiplier=1), reads=[t_ones], writes=[t_msk])
        S.pool(lambda e: e.memset(modT[:], 0.0), writes=[t_modT])
        S.dma("sp", lambda e: e.dma_start(out=g1T[:], in_=I["g1T"]), writes=[t_g1T])
        S.dma("sp", lambda e: e.dma_start(out=g2T[:], in_=I["g2T"]), writes=[t_g2T])

        with ExitStack() as ph:
            def sbp(name, shape, dt=F32):
                return ph.enter_context(nc.sbuf_tensor(un("s_" + name), list(shape), dt))
            cv = sbp("cv", [128, KC, 2]); t_cv = T("cv")
            cond = sbp("cond", [128, KC, 2]); t_cond = T("cond")
            condB = sbp("condB", [128, KC, 128]); t_condB = T("condB")
            bmT = sbp("bmT", [128, 48]); t_bmT = T("bmT")
            bmR = sbp("bmR", [1, 6 * D]); t_bmR = T("bmR")
            wm = [sbp("wm%d" % i, [128, KC, 512]) for i in range(2)]
            t_wm = [T("wm0"), T("wm1")]
            S.dma("sp", lambda e: e.dma_start(out=cv[:], in_=I["cvec"]), writes=[t_cv])
            S.dma("sp", lambda e: e.dma_start(out=bmT[:], in_=I["b_modT"]), writes=[t_bmT])
            S.dma("sp", lambda e: e.dma_start(out=bmR[:], in_=I["b_modR"]), writes=[t_bmR])
            S.act(lambda e: e.activation(out=cond[:], in_=cv[:], func=AF.Silu), reads=[t_cv], writes=[t_cond])
            for kc in range(KC):
                S.dve(lambda e, kc=kc: e.tensor_scalar(out=condB[:, kc, :], in0=ones[:], scalar1=cond[:, kc, 0:1],
                                                       scalar2=None, op0=ALU.mult),
                      reads=[t_ones, t_cond], writes=[t_condB])
            wsrc = I["w_mod"].rearrange("(kc p) n -> p kc n", p=128)
            FMJ = {0: 0, 1: 4, 2: 8, 3: 12, 6: 24, 7: 28, 8: 32, 9: 36}
            RBJ = {4: (0, 0), 5: (0, 1), 10: (1, 0), 11: (1, 1)}
            for j in range(12):
                b = j % 2
                S.dma("sp", lambda e, j=j, b=b: e.dma_start(out=wm[b][:], in_=wsrc[:, :, 512 * j:512 * j + 512]),
                      writes=[t_wm[b]])
                pb = 0 + (j % 2)
                if j in FMJ:
                    for ct in range(4):
                        for kc in range(KC):
                            S.pe(lambda e, ct=ct, kc=kc, b=b, pb=pb: e.matmul(
                                bank(pb)[:, 2 * ct:2 * ct + 2], wm[b][:, kc, 128 * ct:128 * ct + 128], cond[:, kc, :],
                                start=(kc == 0), stop=(kc == KC - 1)),
                                reads=[t_wm[b], t_cond], writes=[t_PB[pb]])
                    j0 = FMJ[j]
                    S.dve(lambda e, pb=pb, j0=j0: e.tensor_tensor(
                        out=modT[:, j0:j0 + 4, :], in0=bank(pb)[:, 0:8].rearrange("p (c m) -> p c m", m=2),
                        in1=bmT[:, j0:j0 + 4].unsqueeze(2).broadcast_to([128, 4, 2]), op=ALU.add),
                        reads=[t_PB[pb], t_bmT], writes=[t_modT])
                else:
                    which, half = RBJ[j]
                    for kc in range(KC):
                        S.pe(lambda e, kc=kc, b=b, pb=pb: e.matmul(
                            bank(pb), condB[:, kc, :], wm[b][:, kc, :], start=(kc == 0), stop=False),
                            reads=[t_wm[b], t_condB], writes=[t_PB[pb]])
                    S.pe(lambda e, j=j, pb=pb: e.matmul(
                        bank(pb), ones[0:1, :], bmR[0:1, 512 * j:512 * j + 512], start=False, stop=True),
                        reads=[t_ones, t_bmR], writes=[t_PB[pb]])
                    S.act(lambda e, pb=pb, which=which, half=half: e.activation(
                        out=gtR[:, which, 512 * half:512 * half + 512], in_=bank(pb), func=AF.Identity),
                        reads=[t_PB[pb]], writes=[t_gtR])
            for (aT, t_aT, gT, t_gT, j0) in ((a1T, t_a1T, g1T, t_g1T, 8), (a2T, t_a2T, g2T, t_g2T, 32)):
                S.dve(lambda e, aT=aT, gT=gT, j0=j0: e.scalar_tensor_tensor(
                    out=aT[:], in0=modT[:, j0:j0 + 8, :], scalar=1.0,
                    in1=gT[:].unsqueeze(2).broadcast_to([128, KC, 2]), op0=ALU.add, op1=ALU.mult),
                    reads=[t_modT, t_gT], writes=[t_aT])
            S.barrier()

        def norm_tiles(ph, jobs, aT, t_aT, sh_j0, pbase):
            xs = [ph.enter_context(nc.sbuf_tensor(un("xs%d" % i), [128, D], F32)) for i in range(2)]
            t_xs = [T("xs0"), T("xs1")]
            xn = [ph.enter_context(nc.sbuf_tensor(un("xn%d" % i), [128, D], F32)) for i in range(2)]
            t_xn = [T("xn0"), T("xn1")]
            junk = ph.enter_context(nc.sbuf_tensor(un("junk"), [128, D], BF16)); t_junk = T("junk")
            tmp = [ph.enter_context(nc.sbuf_tensor(un("ntmp%d" % i), [128, KC, 128], F32)) for i in range(2)]
            t_tmp = [T("ntmp0"), T("ntmp1")]
            st4 = [ph.enter_context(nc.sbuf_tensor(un("nst%d" % i), [128, 4], F32)) for i in range(2)]
            t_st4 = [T("nst0"), T("nst1")]
            for n, (src, dst, t_dst, m) in enumerate(jobs):
                b = n % 2
                if isinstance(src, tuple):
                    xin, t_xin = src
                else:
                    S.dma("sp", lambda e, src=src, b=b: e.dma_start(out=xs[b][:], in_=src), writes=[t_xs[b]])
                    xin, t_xin = xs[b][:], t_xs[b]
                s4 = st4[b]
                S.act(lambda e, xin=xin, s4=s4: e.activation(out=junk[:], in_=xin, func=AF.Square, accum_out=s4[:, 0:1]),
                      reads=[t_xin], writes=[t_junk, t_st4[b]])
                S.dve(lambda e, s4=s4: e.tensor_scalar(out=s4[:, 1:2], in0=s4[:, 0:1], scalar1=1.0 / D, scalar2=EPS,
                                                       op0=ALU.mult, op1=ALU.add), reads=[t_st4[b]], writes=[t_st4[b]])
                S.act(lambda e, s4=s4: e.activation(out=s4[:, 2:3], in_=s4[:, 1:2], func=AF.Sqrt),
                      reads=[t_st4[b]], writes=[t_st4[b]])
                S.dve(lambda e, s4=s4: e.reciprocal(out=s4[:, 3:4], in_=s4[:, 2:3]), reads=[t_st4[b]], writes=[t_st4[b]])
                S.act(lambda e, xin=xin, s4=s4, b=b: e.activation(out=xn[b][:], in_=xin, func=AF.Identity, scale=s4[:, 3:4]),
                      reads=[t_xin, t_st4[b]], writes=[t_xn[b]])
                pp = pbase + b
                for kc in range(KC):
                    S.pe(lambda e, kc=kc, b=b, pp=pp: e.transpose(out=PP[pp][:, 128 * kc:128 * kc + 128],
                                                                  in_=xn[b][:, 128 * kc:128 * kc + 128], identity=ident[:]),
                         reads=[t_xn[b], t_ident], writes=[t_PB[2 * pp + kc // 4]])
                S.dve(lambda e, b=b, pp=pp, m=m: e.tensor_tensor(
                    out=tmp[b][:], in0=PP[pp][:].rearrange("p (c t) -> p c t", t=128),
                    in1=aT[:, :, m:m + 1].broadcast_to([128, KC, 128]), op=ALU.mult),
                    reads=[t_PB[2 * pp], t_PB[2 * pp + 1], t_aT], writes=[t_tmp[b]])
                S.pool(lambda e, b=b, dst=dst, m=m: e.tensor_tensor(
                    out=dst, in0=tmp[b][:], in1=modT[:, sh_j0:sh_j0 + 8, m:m + 1].broadcast_to([128, KC, 128]), op=ALU.add),
                    reads=[t_tmp[b], t_modT], writes=[t_dst])

        with ExitStack() as ph:
            jobs = []
            for i in range(2):
                jobs.append((I["cx"][128 * i:128 * i + 128, :], hTc[:, :, 128 * i:128 * i + 128], t_hTc[i], 1))
            for i in range(16 if stage != "ffn_only" else 0):
                jobs.append((I["xt"][128 * i:128 * i + 128, :], hTt[:, :, 128 * i:128 * i + 128], t_hT[16 + i], 0))
            for i in range(16):
                jobs.append((I["xo"][128 * i:128 * i + 128, :], hT[:, :, 128 * i:128 * i + 128], t_hT[i], 0))
            norm_tiles(ph, jobs, a1T, t_a1T, 0, 1)
            S.barrier()

        OGD = None
        if stage in ("full", "delta"):
            OGD = k.dram_out("d_og", [128, 8, HALF], BF16) if dbg else nc.dram_tensor("ogd_scratch", [128, 8, HALF], BF16).ap()
        if stage in ("full", "delta"):
            with ExitStack() as ph:
                def sbp(name, shape, dt=F32):
                    return ph.enter_context(nc.sbuf_tensor(un("s_" + name), list(shape), dt))
                _rr = {}

                def ring(name, n, shape, dt=F32):
                    tiles = [(sbp("%s%d" % (name, i), shape, dt), T("%s%d" % (name, i))) for i in range(n)]
                    _rr[name] = [tiles, 0]

                def nxt(name):
                    tiles, i = _rr[name]
                    _rr[name][1] = i + 1
                    return tiles[i % len(tiles)]
                _pb = [0]

                def nbank():
                    i = _pb[0] % 8
                    _pb[0] += 1
                    return bank(i), t_PB[i]

                def mm(o, lhsT, rhs, r, w, start=True, stop=True):
                    S.pe(lambda e: e.matmul(o, lhsT, rhs, start=start, stop=stop), reads=r, writes=w)

                def trp(o, in_, r, w):
                    S.pe(lambda e: e.transpose(out=o, in_=in_, identity=ident[:]), reads=list(r) + [t_ident], writes=w)

                def tt(eng, o, a, b, op, r, w):
                    S.op(eng, lambda e: e.tensor_tensor(out=o, in0=a, in1=b, op=op), r, w)

                def actf(o, in_, func, r, w, scale=1.0, bias=None):
                    if bias is None:
                        S.act(lambda e: e.activation(out=o, in_=in_, func=func, scale=scale), r, w)
                    else:
                        S.act(lambda e: e.activation(out=o, in_=in_, func=func, scale=scale, bias=bias), r, w)

                cwT = sbp("cwT", [128, 24, 3]); t_cwT = T("cwT")
                alogR = sbp("alogR", [128, 16]); dtbR = sbp("dtbR", [128, 16]); negA = sbp("negA", [128, 16])
                ongT = sbp("ongT", [128, 1]); t_cst = T("dcst")
                S.dma("sp", lambda e: e.dma_start(out=cwT[:], in_=I["cwT"]), writes=[t_cwT])
                S.dma("sp", lambda e: e.dma_start(out=alogR[:], in_=I["alogR"]), writes=[t_cst])
                S.dma("sp", lambda e: e.dma_start(out=dtbR[:], in_=I["dtbR"]), writes=[t_cst])
                S.dma("sp", lambda e: e.dma_start(out=ongT[:], in_=I["ongT"]), writes=[t_cst])
                actf(negA[:], alogR[:], AF.Exp, [t_cst], [t_cst])
                S.dve(lambda e: e.tensor_scalar(out=negA[:], in0=negA[:], scalar1=-1.0, scalar2=None, op0=ALU.mult),
                      [t_cst], [t_cst])
                NTL = 34
                names = ["BETA", "GC", "EKD", "EGEND", "BSC", "NB", "NGC"]
                SC = {n: sbp("sc_" + n, [128, NTL, 16]) for n in names}

                t_SC = T("SC")
                wba = sbp("wba", [128, KC, 32], BF16); t_wba = T("wba")
                winsrc = I["w_in"].rearrange("(kc p) n -> p kc n", p=128)
                stgD = [(sbp("stgD%d" % i, [128, 1024]), T("stgD%d" % i)) for i in range(1)]
                load_cast(stgD, wba[:], t_wba, winsrc[:, :, OFF_BA:OFF_BA + 32], KC * 32)
                ring("sct", 2, [128, 16])
                sctmp = ExitStack()
                for n in ["GEND", "EG", "GG"]:
                    SC[n] = sctmp.enter_context(nc.sbuf_tensor(un("s_sc_" + n), [128, NTL, 16], F32))

                def h_tile(ti):
                    if ti < 16:
                        return hT[:, :, 128 * ti:128 * ti + 128], t_hT[ti]
                    if ti < 32:
                        return hTt[:, :, 128 * (ti - 16):128 * (ti - 16) + 128], t_hT[ti]
                    return hTc[:, :, 128 * (ti - 32):128 * (ti - 32) + 128], t_hTc[ti - 32]
                for ti in range(NTL):
                    ha, t_ha = h_tile(ti)
                    pb, t_pb = nbank()
                    for kc in range(KC):
                        mm(pb[:, 0:32], ha[:, kc, :], wba[:, kc, :], [t_ha, t_wba], [t_pb], start=(kc == 0), stop=(kc == KC - 1))
                    actf(SC["BETA"][:, ti, :], pb[:, 0:16], AF.Sigmoid, [t_pb], [t_SC])
                    tmp, t_tmp = nxt("sct")
                    tt("dve", tmp[:], pb[:, 16:32], dtbR[:], ALU.add, [t_pb, t_cst], [t_tmp])
                    actf(tmp[:], tmp[:], AF.Exp, [t_tmp], [t_tmp])
                    actf(tmp[:], tmp[:], AF.Ln, [t_tmp], [t_tmp], bias=1.0)
                    tt("dve", SC["GG"][:, ti, :], tmp[:], negA[:], ALU.mult, [t_tmp, t_cst], [t_SC])
                    pc, t_pc = nbank()
                    mm(pc[:, 0:8], triU[:], SC["GG"][:, ti, 0:8], [t_msk, t_SC], [t_pc])
                    mm(pc[:, 8:16], triL[:], SC["GG"][:, ti, 8:16], [t_msk, t_SC], [t_pc])
                    mm(pc[:, 16:32], ones[:], SC["GG"][:, ti, :], [t_ones, t_SC], [t_pc])
                    actf(SC["GC"][:, ti, :], pc[:, 0:16], AF.Identity, [t_pc], [t_SC])
                    actf(SC["GEND"][:, ti, :], pc[:, 16:32], AF.Identity, [t_pc], [t_SC])
                actf(SC["EG"][:], SC["GC"][:], AF.Exp, [t_SC], [t_SC])
                actf(SC["EGEND"][:], SC["GEND"][:], AF.Exp, [t_SC], [t_SC])
                tt("dve", SC["EKD"][:], SC["GEND"][:], SC["GC"][:], ALU.subtract, [t_SC], [t_SC])
                actf(SC["EKD"][:], SC["EKD"][:], AF.Exp, [t_SC], [t_SC])
                tt("dve", SC["BSC"][:], SC["BETA"][:], SC["EG"][:], ALU.mult, [t_SC], [t_SC])
                S.dve(lambda e: e.tensor_scalar(out=SC["NB"][:], in0=SC["BETA"][:], scalar1=-1.0, scalar2=None, op0=ALU.mult),
                      [t_SC], [t_SC])
                S.dve(lambda e: e.tensor_scalar(out=SC["NGC"][:], in0=SC["GC"][:], scalar1=-1.0, scalar2=None, op0=ALU.mult),
                      [t_SC], [t_SC])
                S.barrier()
                sctmp.close()

                LK = SEQ + CTX
                kn = sbp("kn", [128, LK]); t_kn = T("kn")
                vs = sbp("vs", [128, LK]); t_vs = T("vs")
                raw = X1[:, 8:12, :].rearrange("p a n -> p (a n)"); t_raw = T("raw")
                qn = X1[:, 12:14, :].rearrange("p a n -> p (a n)"); t_qn = T("qn")
                oT = X1[:, 14:16, :].rearrange("p a n -> p (a n)"); t_oT = [T("oT%d" % i) for i in range(NT)]
                rawc = sbp("rawc", [128, CTX]); t_rawc = T("rawc")
                wh = sbp("wh", [128, KC, 4, 128], BF16); t_wh = T("wh")
                Sst = sbp("Sst", [128, 128]); t_S = T("S")
                ogs = sbp("ogs", [128, HALF], BF16); t_ogs = T("ogs")
                ring("m", 28, [128, 128])
                ring("t", 8, [128, 128])
                ring("blk", 3, [128, 512])

                def project(ct, dst, t_dst, segs):
                    for (ha, t_has, c0, n) in segs:
                        pb, t_pb = nbank()
                        for kc in range(KC):
                            mm(pb[:, 0:n], wh[:, kc, ct, :], ha[:, kc, :], [t_wh] + list(t_has), [t_pb],
                               start=(kc == 0), stop=(kc == KC - 1))
                        actf(dst[:, c0:c0 + n], pb[:, 0:n], AF.Identity, [t_pb], [t_dst])

                def conv_silu(src, t_src, dst, t_dst, L_in, L_out, cwi):
                    actf(dst[:, 0:L_out], src[:, 0:L_out], AF.Identity, [t_src, t_cwT], [t_dst], scale=cwT[:, cwi, 1:2])
                    S.dve(lambda e: e.scalar_tensor_tensor(out=dst[:, 1:L_out], in0=src[:, 0:L_out - 1], scalar=cwT[:, cwi, 0:1],
                                                           in1=dst[:, 1:L_out], op0=ALU.mult, op1=ALU.add),
                          [t_src, t_cwT, t_dst], [t_dst])
                    hi = min(L_out, L_in - 1)
                    S.dve(lambda e: e.scalar_tensor_tensor(out=dst[:, 0:hi], in0=src[:, 1:hi + 1], scalar=cwT[:, cwi, 2:3],
                                                           in1=dst[:, 0:hi], op0=ALU.mult, op1=ALU.add),
                          [t_src, t_cwT, t_dst], [t_dst])
                    actf(dst[:, 0:L_out], dst[:, 0:L_out], AF.Silu, [t_dst], [t_dst])

                def l2n(buf, t_buf, c0, n, mult):
                    sq, t_sq = nxt("blk")
                    actf(sq[:, 0:n], buf[:, c0:c0 + n], AF.Square, [t_buf], [t_sq])
                    pb, t_pb = nbank()
                    mm(pb[:, 0:n], ones[:], sq[:, 0:n], [t_ones, t_sq], [t_pb])
                    r1, t_r1 = nxt("blk")
                    S.dve(lambda e: e.tensor_scalar(out=r1[:, 0:n], in0=pb[:, 0:n], scalar1=EPS, scalar2=None, op0=ALU.add),
                          [t_pb], [t_r1])
                    actf(r1[:, 0:n], r1[:, 0:n], AF.Ln, [t_r1], [t_r1])
                    actf(r1[:, 0:n], r1[:, 0:n], AF.Exp, [t_r1], [t_r1], scale=-0.5)
                    S.dve(lambda e: e.scalar_tensor_tensor(out=buf[:, c0:c0 + n], in0=buf[:, c0:c0 + n], scalar=float(mult),
                                                           in1=r1[:, 0:n], op0=ALU.mult, op1=ALU.mult),
                          [t_buf, t_r1], [t_buf])

                seg_own = [(hT[:, :, 512 * b:512 * b + 512], t_hT[4 * b:4 * b + 4], 512 * b, 512) for b in range(4)]
                seg_oth = [(hTt[:, :, 512 * b:512 * b + 512], t_hT[16 + 4 * b:16 + 4 * b + 4], HALF + 512 * b, 512) for b in range(4)]
                seg_ctx = [(hTc[:, :, :], t_hTc, 0, CTX)]
                seg_q2 = [(hTt[:, :, 0:2], t_hT[16:17], HALF, 2)]

                def unit(h, d, ti, full):
                    col = d * 8 + h
                    t0 = 128 * ti if ti < 32 else SEQ + 128 * (ti - 32)
                    kTc = kn[:, t0:t0 + 128]
                    vTc = vs[:, t0:t0 + 128]
                    Tri_, inclT_, strict_ = (triU, triU, strL) if d == 0 else (triL, triL, strU)
                    sc = lambda n: SC[n][:, ti, col:col + 1]
                    pa, t_pa = nbank()
                    trp(pa[:, 0:128], kTc, [t_kn], [t_pa])
                    UVAR = int(os.environ.get("K_UVAR", 9))
                    kbg, t_kbg = nxt("m")
                    if UVAR >= 2:
                        actf(kbg[:], pa[:, 0:128], AF.Identity, [t_pa, t_SC], [t_kbg], scale=sc("BSC"))
                    kd, t_kd = nxt("m")
                    if UVAR >= 3:
                        S.dve(lambda e: e.tensor_scalar(out=kd[:], in0=pa[:, 0:128], scalar1=sc("EKD"), scalar2=None, op0=ALU.mult),
                              [t_pa, t_SC], [t_kd])
                    pv, t_pv = nbank()
                    trp(pv[:, 0:128], vTc, [t_vs], [t_pv])
                    vb, t_vb = nxt("m")
                    if UVAR >= 2:
                        actf(vb[:], pv[:, 0:128], AF.Identity, [t_pv, t_SC], [t_vb], scale=sc("BETA"))
                    USTOP = int(os.environ.get("K_USTOP", 9))
                    if USTOP < 2:
                        return
                    dg, t_dg = nxt("m")
                    S.dve(lambda e: e.tensor_scalar(out=dg[:], in0=ident[:], scalar1=sc("GC"), scalar2=None, op0=ALU.mult),
                          [t_ident, t_SC], [t_dg])
                    pr, t_pr = nbank()
                    mm(pr[:, 0:128], ones[:], dg[:], [t_ones, t_dg], [t_pr])
                    E, t_E = nxt("m")
                    actf(E[:], pr[:, 0:128], AF.Abs, [t_pr, t_SC], [t_E], bias=sc("NGC"))
                    actf(E[:], E[:], AF.Exp, [t_E], [t_E], scale=-1.0)
                    if USTOP < 3:
                        return
                    Es, t_Es = nxt("m")
                    tt("pool", Es[:], E[:], strict_[:], ALU.mult, [t_E, t_msk], [t_Es])
                    pk, t_pk = nbank()
                    mm(pk[:, 0:128], kTc, kTc, [t_kn], [t_pk])
                    Nm, t_N = nxt("m")
                    S.dve(lambda e, Nm=Nm: e.scalar_tensor_tensor(out=Nm[:], in0=pk[:, 0:128], scalar=sc("NB"), in1=Es[:],
                                                                  op0=ALU.mult, op1=ALU.mult), [t_pk, t_SC, t_Es], [t_N])
                    if full:
                        qTc = qn[:, 128 * ti:128 * ti + 128]
                        egr, t_egr = nxt("m")
                        actf(egr[:], pr[:, 0:128], AF.Exp, [t_pr], [t_egr])
                        qg, t_qg = nxt("m")
                        tt("pool", qg[:], qTc, egr[:], ALU.mult, [t_qn, t_egr], [t_qg])
                        Ei, t_Ei = nxt("m")
                        tt("pool", Ei[:], E[:], inclT_[:], ALU.mult, [t_E, t_msk], [t_Ei])
                        pq, t_pq = nbank()
                        mm(pq[:, 0:128], kTc, qTc, [t_kn, t_qn], [t_pq])
                        qk, t_qk = nxt("m")
                        tt("dve", qk[:], pq[:, 0:128], Ei[:], ALU.mult, [t_pq, t_Ei], [t_qk])
                    if USTOP < 4:
                        return
                    pt, t_pt = nbank()
                    trp(pt[:, 0:128], Nm[:], [t_N], [t_pt])
                    Nt, t_Nt = nxt("t")
                    actf(Nt[:], pt[:, 0:128], AF.Identity, [t_pt], [t_Nt])
                    Xt, t_Xt = nxt("t")
                    tt("dve", Xt[:], pt[:, 0:128], ident[:], ALU.add, [t_pt, t_ident], [t_Xt])
                    for lvl in range(6):
                        p1, t_p1 = nbank()
                        mm(p1[:, 0:128], Nt[:], Nm[:], [t_Nt, t_N], [t_p1])
                        N2, t_N2 = nxt("t")
                        actf(N2[:], p1[:, 0:128], AF.Identity, [t_p1], [t_N2])
                        if lvl < 5:
                            p2, t_p2 = nbank()
                            mm(p2[:, 0:128], Nm[:], Nt[:], [t_Nt, t_N], [t_p2])
                            Nt2, t_Nt2 = nxt("t")
                            S.dve(lambda e, Nt2=Nt2, p2=p2: e.tensor_copy(out=Nt2[:], in_=p2[:, 0:128]), [t_p2], [t_Nt2])
                        p3, t_p3 = nbank()
                        mm(p3[:, 0:128], N2[:], Xt[:], [t_N2, t_Xt], [t_p3])
                        X2, t_X2 = nxt("t")
                        tt("dve", X2[:], p3[:, 0:128], Xt[:], ALU.add, [t_p3, t_Xt], [t_X2])
                        Nm, t_N = N2, t_N2
                        if lvl < 5:
                            Nt, t_Nt = Nt2, t_Nt2
                        Xt, t_Xt = X2, t_X2
                    if USTOP < 5:
                        return
                    pw, t_pw = nbank()
                    mm(pw[:, 0:128], kbg[:], Xt[:], [t_kbg, t_Xt], [t_pw])
                    wT, t_wT = nxt("m")
                    actf(wT[:], pw[:, 0:128], AF.Identity, [t_pw], [t_wT])
                    pu, t_pu = nbank()
                    mm(pu[:, 0:128], Xt[:], vb[:], [t_Xt, t_vb], [t_pu])
                    u, t_u = nxt("m")
                    actf(u[:], pu[:, 0:128], AF.Identity, [t_pu], [t_u])
                    if USTOP < 6:
                        return
                    pp1, t_pp1 = nbank()
                    mm(pp1[:, 0:128], wT[:], Sst[:], [t_wT, t_S], [t_pp1])
                    vn, t_vn = nxt("m")
                    tt("dve", vn[:], u[:], pp1[:, 0:128], ALU.subtract, [t_u, t_pp1], [t_vn])
                    if full:
                        po, t_po = nbank()
                        mm(po[:, 0:128], Sst[:], qg[:], [t_S, t_qg], [t_po], start=True, stop=False)
                        mm(po[:, 0:128], vn[:], qk[:], [t_vn, t_qk], [t_po], start=False, stop=True)
                        osl = oT[:, 128 * ti:128 * ti + 128]
                        if d == 0:
                            actf(osl, po[:, 0:128], AF.Identity, [t_po], [t_oT[ti]])
                        else:
                            tt("dve", osl, po[:, 0:128], osl, ALU.add, [t_po, t_oT[ti]], [t_oT[ti]])
                    ps_, t_ps = nbank()
                    mm(ps_[:, 0:128], kd[:], vn[:], [t_kd, t_vn], [t_ps])
                    S.dve(lambda e: e.scalar_tensor_tensor(out=Sst[:], in0=Sst[:], scalar=sc("EGEND"), in1=ps_[:, 0:128],
                                                           op0=ALU.mult, op1=ALU.add), [t_S, t_SC, t_ps], [t_S])

                NH_ = int(os.environ.get("K_NH", 8))
                DSTOP = int(os.environ.get("K_DSTOP", 9))
                for h in range(NH_ if DSTOP > 0 else 0):
                    for ci, off in enumerate((OFF_K, OFF_V, OFF_Q, OFF_Z)):
                        load_cast(stgD, wh[:, :, ci, :], t_wh, winsrc[:, :, off + 128 * h:off + 128 * h + 128], KC * 128)
                    project(0, raw, t_raw, seg_own + seg_oth)
                    project(0, rawc, t_rawc, seg_ctx)
                    conv_silu(raw, t_raw, kn, t_kn, SEQ, SEQ, h)
                    conv_silu(rawc, t_rawc, kn[:, SEQ:LK], t_kn, CTX, CTX, h)
                    for c0 in range(0, LK, 512):
                        l2n(kn, t_kn, c0, min(512, LK - c0), 1.0)
                    project(1, raw, t_raw, seg_own + seg_oth)
                    project(1, rawc, t_rawc, seg_ctx)
                    conv_silu(raw, t_raw, vs, t_vs, SEQ, SEQ, 8 + h)
                    conv_silu(rawc, t_rawc, vs[:, SEQ:LK], t_vs, CTX, CTX, 8 + h)
                    project(2, raw, t_raw, seg_own + seg_q2)
                    conv_silu(raw, t_raw, qn, t_qn, HALF + 1, HALF, 16 + h)
                    for c0 in range(0, HALF, 512):
                        l2n(qn, t_qn, c0, 512, 128.0 ** -0.5)
                    for d in range(2 if DSTOP > 1 else 0):
                        S.dve(lambda e: e.memset(Sst[:], 0.0), [], [t_S])
                        order = [32, 33] if d == 0 else [33, 32] + list(range(31, 15, -1))
                        for ti in order:
                            unit(h, d, ti, False)
                        for ti in ((range(16) if d == 0 else range(15, -1, -1)) if DSTOP > 2 else []):
                            unit(h, d, ti, True)
                    for b in range(4 if not int(os.environ.get("K_NOGATE", 0)) else 0):
                        osl = oT[:, 512 * b:512 * b + 512]
                        t_os = t_oT[4 * b:4 * b + 4]
                        sq, t_sq = nxt("blk")
                        actf(sq[:], osl, AF.Square, t_os, [t_sq])
                        pb, t_pb = nbank()
                        mm(pb, ones[:], sq[:], [t_ones, t_sq], [t_pb])
                        r1, t_r1 = nxt("blk")
                        S.dve(lambda e, r1=r1, pb=pb: e.tensor_scalar(out=r1[:], in0=pb, scalar1=1.0 / 128, scalar2=EPS,
                                                                    op0=ALU.mult, op1=ALU.add), [t_pb], [t_r1])
                        actf(r1[:], r1[:], AF.Ln, [t_r1], [t_r1])
                        actf(r1[:], r1[:], AF.Exp, [t_r1], [t_r1], scale=-0.5)
                        S.dve(lambda e, r1=r1, osl=osl: e.scalar_tensor_tensor(out=r1[:], in0=osl, scalar=ongT[:, 0:1], in1=r1[:],
                                                                             op0=ALU.mult, op1=ALU.mult),
                              list(t_os) + [t_r1, t_cst], [t_r1])
                        pz, t_pz = nbank()
                        for kc in range(KC):
                            mm(pz, wh[:, kc, 3, :], hT[:, kc, 512 * b:512 * b + 512], [t_wh] + t_hT[4 * b:4 * b + 4], [t_pz],
                               start=(kc == 0), stop=(kc == KC - 1))
                        zs, t_zs = nxt("blk")
                        actf(zs[:], pz, AF.Silu, [t_pz], [t_zs])
                        tt("pool", ogs[:, 512 * b:512 * b + 512], r1[:], zs[:], ALU.mult, [t_r1, t_zs], [t_ogs])
                    if not int(os.environ.get("K_NOOGD", 0)):
                        S.dma("sp", lambda e, h=h: e.dma_start(out=OGD[:, h, :], in_=ogs[:]), reads=[t_ogs])
                if dbg and not int(os.environ.get("K_NODBG", 0)):
                    S.dma("sp", lambda e: e.dma_start(out=dbgo["kn"], in_=kn[:]), reads=[t_kn])
                    S.dma("sp", lambda e: e.dma_start(out=dbgo["vs"], in_=vs[:]), reads=[t_vs])
                    S.dma("sp", lambda e: e.dma_start(out=dbgo["qn"], in_=qn), reads=[t_qn])
                    S.dma("sp", lambda e: e.dma_start(out=dbgo["oT"], in_=oT), reads=t_oT)
                    S.dma("sp", lambda e: e.dma_start(out=dbgo["beta"], in_=SC["BETA"][:]), reads=[t_SC])
                    S.dma("sp", lambda e: e.dma_start(out=dbgo["gc"], in_=SC["GC"][:]), reads=[t_SC])
                    S.dma("sp", lambda e: e.dma_start(out=dbgo["Sst"], in_=Sst[:]), reads=[t_S])
                S.barrier()

        if stage == "full":
            YAD = nc.dram_tensor("yad_scratch", [128, 8, HALF], BF16).ap()
            winsrc = I["w_in"].rearrange("(kc p) n -> p kc n", p=128)
            _pb2 = [0]

            def nbank2():
                i = _pb2[0] % 8
                _pb2[0] += 1
                return bank(i), t_PB[i]

            def mm2(o, lhsT, rhs, r, w, start=True, stop=True):
                S.pe(lambda e: e.matmul(o, lhsT, rhs, start=start, stop=stop), reads=r, writes=w)

            def merge_pass(SRC, wproj, gate_off, init_x):
                with ExitStack() as ph:
                    def sbp(name, shape, dt=F32):
                        return ph.enter_context(nc.sbuf_tensor(un("s_" + name), list(shape), dt))
                    stg = [(sbp("stgP%d" % i, [128, 2048]), T("stgP%d" % i)) for i in range(2)]
                    Wp = sbp("Wp", [128, KC, D], BF16); t_Wp = T("Wp")
                    Wg = sbp("Wg", [128, KC, D], BF16); t_Wg = T("Wg")
                    Wo = sbp("Wo", [128, KC, D], BF16); t_Wo = T("Wo")
                    psrc = I[wproj].rearrange("(kc p) n -> p kc n", p=128)
                    osrc = I["w_out"].rearrange("(kc p) n -> p kc n", p=128)
                    for c in range(4):
                        load_cast(stg, Wp[:, 2 * c:2 * c + 2, :], t_Wp, psrc[:, 2 * c:2 * c + 2, :], 2048)
                        load_cast(stg, Wg[:, 2 * c:2 * c + 2, :], t_Wg, winsrc[:, 2 * c:2 * c + 2, gate_off:gate_off + D], 2048)
                        (sa, t_sa) = stg[c % 2]
                        sv = sa[:, 0:2048].rearrange("p (a b) -> p a b", b=D)
                        S.dma("sp", lambda e, sv=sv, c=c: e.dma_start(out=sv, in_=osrc[:, 2 * c:2 * c + 2, :]), writes=[t_sa])
                        S.pool(lambda e, sv=sv, c=c: e.tensor_tensor(
                            out=Wo[:, 2 * c:2 * c + 2, :], in0=sv, in1=gtR[:, 0:1, :].broadcast_to([128, 2, D]), op=ALU.mult),
                            [t_sa, t_gtR], [t_Wo])
                    if init_x:
                        for i in range(NT):
                            S.dma("sp", lambda e, i=i: e.dma_start(out=X1[:, i, :], in_=I["xo"][128 * i:128 * i + 128, :]),
                                  writes=[t_x1[i]])
                    inb = [(sbp("inb%d" % i, [128, KC, 512], BF16), T("inb%d" % i)) for i in range(1)] * 2
                    mbl = [(sbp("mbl%d" % i, [128, KC, 512], BF16), T("mbl%d" % i)) for i in range(2)]
                    sg = [(sbp("sg%d" % i, [128, 512]), T("sg%d" % i)) for i in range(2)]
                    for tb in range(4):
                        (ib, t_ib) = inb[tb % 2]
                        (mb, t_mb) = mbl[tb % 2]
                        S.dma("sp", lambda e, ib=ib, tb=tb: e.dma_start(out=ib[:], in_=SRC[:, :, 512 * tb:512 * tb + 512]), writes=[t_ib])
                        for ct in range(8):
                            pa, t_pa = nbank2()
                            for kc in range(KC):
                                mm2(pa, Wp[:, kc, 128 * ct:128 * ct + 128], ib[:, kc, :], [t_Wp, t_ib], [t_pa],
                                    start=(kc == 0), stop=(kc == KC - 1))
                            pg, t_pg = nbank2()
                            for kc in range(KC):
                                mm2(pg, Wg[:, kc, 128 * ct:128 * ct + 128], hT[:, kc, 512 * tb:512 * tb + 512],
                                    [t_Wg] + t_hT[4 * tb:4 * tb + 4], [t_pg], start=(kc == 0), stop=(kc == KC - 1))
                            (sgt, t_sg) = sg[ct % 2]
                            S.act(lambda e, sgt=sgt, pg=pg: e.activation(out=sgt[:], in_=pg, func=AF.Sigmoid), [t_pg], [t_sg])
                            S.dve(lambda e, mb=mb, ct=ct, pa=pa, sgt=sgt: e.tensor_tensor(
                                out=mb[:, ct, :], in0=pa, in1=sgt[:], op=ALU.mult), [t_pa, t_sg], [t_mb])
                        for tt_ in range(4):
                            ti = 4 * tb + tt_
                            for hh in range(2):
                                pc, t_pc = nbank2()
                                for kc in range(KC):
                                    mm2(pc, mb[:, kc, 128 * tt_:128 * tt_ + 128], Wo[:, kc, 512 * hh:512 * hh + 512],
                                        [t_mb, t_Wo], [t_pc], start=(kc == 0), stop=(kc == KC - 1))
                                S.dve(lambda e, ti=ti, hh=hh, pc=pc: e.tensor_tensor(
                                    out=X1[:, ti, 512 * hh:512 * hh + 512], in0=pc, in1=X1[:, ti, 512 * hh:512 * hh + 512],
                                    op=ALU.add), [t_pc, t_x1[ti]], [t_x1[ti]])
                    S.barrier()

            merge_pass(OGD, "w_proj_b", OFF_GB, True)

            with ExitStack() as ph:
                def sbp(name, shape, dt=F32):
                    return ph.enter_context(nc.sbuf_tensor(un("s_" + name), list(shape), dt))
                Wu = sbp("Wu", [128, KC, D], BF16); t_Wu = T("Wu")
                Wv = sbp("Wv", [128, KC, D], BF16); t_Wv = T("Wv")
                with ExitStack() as ph2:
                    stg = [(ph2.enter_context(nc.sbuf_tensor(un("stgA%d" % i), [128, 2048], F32)), T("stgA%d" % i)) for i in range(2)]
                    for c in range(4):
                        load_cast(stg, Wu[:, 2 * c:2 * c + 2, :], t_Wu, winsrc[:, 2 * c:2 * c + 2, OFF_U:OFF_U + D], 2048)
                        load_cast(stg, Wv[:, 2 * c:2 * c + 2, :], t_Wv, winsrc[:, 2 * c:2 * c + 2, OFF_AV:OFF_AV + D], 2048)
                    S.barrier()
                wsf = sbp("wsf", [128, 8, 128]); wsb = sbp("wsb", [128, 8, 128], BF16); t_ws = T("ws")
                lgR = sbp("lgR", [128, D]); t_lg = T("lgR")
                lbrow = sbp("lbrow", [1, D]); bsrow = sbp("bsrow", [1, D]); rsum = sbp("rsum", [1, D]); t_rows = T("rows")
                S.dma("sp", lambda e: e.dma_start(out=wsf[:], in_=I["wsT"]), writes=[t_ws])
                S.dma("sp", lambda e: e.dma_start(out=lgR[:], in_=I["lgR"]), writes=[t_lg])
                S.dma("sp", lambda e: e.dma_start(out=lbrow[:], in_=I["lbrow"]), writes=[t_rows])
                S.dma("sp", lambda e: e.dma_start(out=bsrow[:], in_=I["bsrow"]), writes=[t_rows])
                S.dve(lambda e: e.tensor_copy(out=wsb[:], in_=wsf[:]), [t_ws], [t_ws])
                for hh in range(2):
                    pr, t_pr = nbank2()
                    mm2(pr[0:1, :], ones[:, 0:1], wsf[:, 4 * hh:4 * hh + 4, :].rearrange("p a b -> p (a b)"), [t_ones, t_ws], [t_pr])
                    S.act(lambda e, pr=pr, hh=hh: e.activation(out=rsum[0:1, 512 * hh:512 * hh + 512], in_=pr[0:1, :], func=AF.Identity),
                          [t_pr], [t_rows])
                gv = sbp("gv", [128, D]); t_gv = T("gv")
                gj = sbp("gj", [128, D], BF16); t_gj = T("gj")
                vt = sbp("vt", [128, D]); t_vt = T("vt")
                vnb = [(sbp("vnb%d" % i, [128, D], BF16), T("vnb%d" % i)) for i in range(2)]
                gu = sbp("gu", [128, D]); t_gu = T("gu")
                yab = [(sbp("yab%d" % i, [128, KC, 512], BF16), T("yab%d" % i)) for i in range(1)] * 2
                st8 = [(sbp("ast%d" % i, [128, 8]), T("ast%d" % i)) for i in range(2)]
                for ti in range(NT):
                    tb, tt_ = ti // 4, ti % 4
                    (ya, t_ya) = yab[tb % 2]
                    (s8, t_s8) = st8[ti % 2]
                    (vn_, t_vn) = vnb[ti % 2]
                    tsl = slice(128 * ti, 128 * ti + 128)
                    pvs = []
                    for hh in range(2):
                        pv, t_pv = nbank2()
                        for kc in range(KC):
                            mm2(pv, hT[:, kc, tsl], Wv[:, kc, 512 * hh:512 * hh + 512], [t_hT[ti], t_Wv], [t_pv],
                                start=(kc == 0), stop=(kc == KC - 1))
                        S.act(lambda e, pv=pv, hh=hh: e.activation(out=gv[:, 512 * hh:512 * hh + 512], in_=pv, func=AF.Gelu_apprx_tanh),
                              [t_pv], [t_gv])
                    S.act(lambda e, s8=s8: e.activation(out=gj[:], in_=gv[:], func=AF.Identity, accum_out=s8[:, 0:1]),
                          [t_gv], [t_gj, t_s8])
                    S.act(lambda e, s8=s8: e.activation(out=gj[:], in_=gv[:], func=AF.Square, accum_out=s8[:, 1:2]),
                          [t_gv], [t_gj, t_s8])
                    S.dve(lambda e, s8=s8: e.tensor_scalar(out=s8[:, 2:3], in0=s8[:, 0:1], scalar1=1.0 / D, scalar2=None, op0=ALU.mult),
                          [t_s8], [t_s8])
                    S.dve(lambda e, s8=s8: e.tensor_tensor(out=s8[:, 3:4], in0=s8[:, 2:3], in1=s8[:, 2:3], op=ALU.mult), [t_s8], [t_s8])
                    S.dve(lambda e, s8=s8: e.scalar_tensor_tensor(out=s8[:, 4:5], in0=s8[:, 1:2], scalar=1.0 / D, in1=s8[:, 3:4],
                                                                  op0=ALU.mult, op1=ALU.subtract), [t_s8], [t_s8])
                    S.dve(lambda e, s8=s8: e.tensor_scalar(out=s8[:, 4:5], in0=s8[:, 4:5], scalar1=EPS, scalar2=None, op0=ALU.add),
                          [t_s8], [t_s8])
                    S.act(lambda e, s8=s8: e.activation(out=s8[:, 5:6], in_=s8[:, 4:5], func=AF.Sqrt), [t_s8], [t_s8])
                    S.dve(lambda e, s8=s8: e.reciprocal(out=s8[:, 6:7], in_=s8[:, 5:6]), [t_s8], [t_s8])
                    S.dve(lambda e, s8=s8: e.scalar_tensor_tensor(out=s8[:, 7:8], in0=s8[:, 2:3], scalar=-1.0, in1=s8[:, 6:7],
                                                                  op0=ALU.mult, op1=ALU.mult), [t_s8], [t_s8])
                    S.act(lambda e, s8=s8: e.activation(out=vt[:], in_=gv[:], func=AF.Identity, scale=s8[:, 6:7], bias=s8[:, 7:8]),
                          [t_gv, t_s8], [t_vt])
                    S.pool(lambda e, vn_=vn_: e.tensor_tensor(out=vn_[:], in0=vt[:], in1=lgR[:], op=ALU.mult), [t_vt, t_lg], [t_vn])
                    for hh in range(2):
                        pu, t_pu = nbank2()
                        for c4 in range(4):
                            ct = 4 * hh + c4
                            for kc in range(KC):
                                mm2(pu[:, 128 * c4:128 * c4 + 128], Wu[:, kc, 128 * ct:128 * ct + 128], hT[:, kc, tsl],
                                    [t_Wu, t_hT[ti]], [t_pu], start=(kc == 0), stop=(kc == KC - 1))
                        S.act(lambda e, pu=pu, hh=hh: e.activation(out=gu[:, 512 * hh:512 * hh + 512], in_=pu, func=AF.Gelu_apprx_tanh),
                              [t_pu], [t_gu])
                    for hh in range(2):
                        pm, t_pm = nbank2()
                        for c4 in range(4):
                            g = 4 * hh + c4
                            osl = pm[:, 128 * c4:128 * c4 + 128]
                            mm2(osl, vn_[:, 128 * g:128 * g + 128], wsb[:, g, :], [t_vn, t_ws], [t_pm], start=True, stop=False)
                            mm2(osl, lbrow[0:1, 128 * g:128 * g + 128], rsum[0:1, 128 * g:128 * g + 128], [t_rows], [t_pm],
                                start=False, stop=False)
                            mm2(osl, ones[0:1, :], bsrow[0:1, 128 * g:128 * g + 128], [t_ones, t_rows], [t_pm], start=False, stop=True)
                        S.dve(lambda e, ya=ya, hh=hh, tt_=tt_, pm=pm: e.tensor_tensor(
                            out=ya[:, 4 * hh:4 * hh + 4, 128 * tt_:128 * tt_ + 128], in0=pm.rearrange("p (a b) -> p a b", b=128),
                            in1=gu[:, 512 * hh:512 * hh + 512].rearrange("p (a b) -> p a b", b=128), op=ALU.mult),
                            [t_pm, t_gu], [t_ya])
                    if tt_ == 3:
                        S.dma("sp", lambda e, ya=ya, tb=tb: e.dma_start(out=YAD[:, :, 512 * tb:512 * tb + 512], in_=ya[:]), reads=[t_ya])
                S.barrier()

            merge_pass(YAD, "w_proj_a", OFF_GA, False)

        if stage == "ffn_only":
            for i in range(NT):
                S.dma("sp", lambda e, i=i: e.dma_start(out=X1[:, i, :], in_=I["xo"][128 * i:128 * i + 128, :]),
                      writes=[t_x1[i]])

        if stage in ("full", "ffn_only"):
            with ExitStack() as ph:
                jobs = [((X1[:, i, :], t_x1[i]), hT[:, :, 128 * i:128 * i + 128], t_hT[i], 0) for i in range(NT)]
                norm_tiles(ph, jobs, a2T, t_a2T, 24, 1)
                S.barrier()
            with ExitStack() as ph:
                def sbp(name, shape, dt=F32):
                    return ph.enter_context(nc.sbuf_tensor(un("s_" + name), list(shape), dt))
                gfR = sbp("gfR", [128, D]); t_gfR = T("gfR")
                fcwT = sbp("fcwT", [128, NFF, 3]); t_fcwT = T("fcwT")
                fcbT = sbp("fcbT", [128, NFF]); t_fcbT = T("fcbT")
                S.dma("sp", lambda e: e.dma_start(out=gfR[:], in_=I["gfR"]), writes=[t_gfR])
                S.dma("sp", lambda e: e.dma_start(out=fcwT[:], in_=I["fcwT"]), writes=[t_fcwT])
                S.dma("sp", lambda e: e.dma_start(out=fcbT[:], in_=I["fcbT"]), writes=[t_fcbT])
                GS = [3, 3, 3, 3, 3, 3, 3, 1]
                G0 = [0, 3, 6, 9, 12, 15, 18, 21]
                GM = 3
                stgF = [(sbp("stgF%d" % i, [128, 2048]), T("stgF%d" % i)) for i in range(2)]
                wu = [sbp("wu%d" % i, [128, KC, 2, GM * 128], BF16) for i in range(2)]
                t_wu = [T("wu0"), T("wu1")]
                wd = [sbp("wd%d" % i, [128, GM, D], BF16) for i in range(2)]
                t_wd = [T("wd0"), T("wd1")]
                actT = [sbp("actT%d" % i, [128, GM, 512], BF16) for i in range(2)]
                t_actT = [T("actT0"), T("actT1")]
                c1 = [sbp("c1_%d" % i, [128, 512]) for i in range(2)]
                t_c1 = [T("c1_0"), T("c1_1")]
                gl = [sbp("gl_%d" % i, [128, 512]) for i in range(2)]
                t_gl = [T("gl_0"), T("gl_1")]
                bsb = [sbp("bsb_%d" % i, [128, 512]) for i in range(2)]
                t_bsb = [T("bsb_0"), T("bsb_1")]
                wusrc = I["w_up"].rearrange("(kc p) n -> p kc n", p=128)
                wdsrc = I["w_down"].rearrange("(j p) n -> p j n", p=128)
                nds = 0
                nu = 0
                NG_ = int(os.environ.get("K_NG", len(GS)))
                NTB_ = int(os.environ.get("K_NTB", 4))
                NOEW_ = int(os.environ.get("K_NOEW", 0))
                for g in range(NG_):
                    G, j0 = GS[g], G0[g]
                    wb = g % 2
                    for ab in range(2):
                        for kh in range(2):
                            load_cast(stgF, wu[wb][:, 4 * kh:4 * kh + 4, ab, 0:G * 128], t_wu[wb],
                                      wusrc[:, 4 * kh:4 * kh + 4, ab * DFF + 128 * j0:ab * DFF + 128 * (j0 + G)], 4 * G * 128)
                    for jj in range(G):
                        (sa, t_sa) = stgF[nds % 2]
                        nds += 1
                        S.dma("sp", lambda e, sa=sa, j=j0 + jj: e.dma_start(out=sa[:, 0:D], in_=wdsrc[:, j, :]), writes=[t_sa])
                        S.pool(lambda e, sa=sa, wb=wb, jj=jj: e.tensor_tensor(
                            out=wd[wb][:, jj, :], in0=sa[:, 0:D], in1=gtR[:, 1, :], op=ALU.mult),
                            reads=[t_sa, t_gtR], writes=[t_wd[wb]])
                    for tb in range(NTB_):
                        ab_ = (g * 4 + tb) % 2
                        for jj in range(G):
                            j = j0 + jj
                            u = nu % 2
                            nu += 1
                            pa, pbk = 0 + u, 2 + u
                            for kc in range(KC):
                                S.pe(lambda e, kc=kc, wb=wb, jj=jj, tb=tb, pa=pa: e.matmul(
                                    bank(pa), wu[wb][:, kc, 0, 128 * jj:128 * jj + 128], hT[:, kc, 512 * tb:512 * tb + 512],
                                    start=(kc == 0), stop=(kc == KC - 1)),
                                    reads=[t_wu[wb]] + t_hT[4 * tb:4 * tb + 4], writes=[t_PB[pa]])
                            for kc in range(KC):
                                S.pe(lambda e, kc=kc, wb=wb, jj=jj, tb=tb, pbk=pbk: e.matmul(
                                    bank(pbk), wu[wb][:, kc, 1, 128 * jj:128 * jj + 128], hT[:, kc, 512 * tb:512 * tb + 512],
                                    start=(kc == 0), stop=(kc == KC - 1)),
                                    reads=[t_wu[wb]] + t_hT[4 * tb:4 * tb + 4], writes=[t_PB[pbk]])
                            S.act(lambda e, u=u, pa=pa, j=j: e.activation(
                                out=c1[u][:], in_=bank(pa), func=AF.Identity, scale=fcwT[:, j, 1:2], bias=fcbT[:, j:j + 1]),
                                reads=[t_PB[pa], t_fcwT, t_fcbT], writes=[t_c1[u]])
                            S.act(lambda e, u=u, pbk=pbk: e.activation(out=bsb[u][:], in_=bank(pbk), func=AF.Identity),
                                  reads=[t_PB[pbk]], writes=[t_bsb[u]])
                            S.dve(lambda e, u=u, pa=pa, j=j: e.scalar_tensor_tensor(
                                out=c1[u][:].rearrange("p (r t) -> p r t", t=64)[:, :, 1:64],
                                in0=bank(pa).rearrange("p (r t) -> p r t", t=64)[:, :, 0:63], scalar=fcwT[:, j, 0:1],
                                in1=c1[u][:].rearrange("p (r t) -> p r t", t=64)[:, :, 1:64], op0=ALU.mult, op1=ALU.add),
                                reads=[t_PB[pa], t_fcwT, t_c1[u]], writes=[t_c1[u]])
                            S.dve(lambda e, u=u, pa=pa, j=j: e.scalar_tensor_tensor(
                                out=c1[u][:].rearrange("p (r t) -> p r t", t=64)[:, :, 0:63],
                                in0=bank(pa).rearrange("p (r t) -> p r t", t=64)[:, :, 1:64], scalar=fcwT[:, j, 2:3],
                                in1=c1[u][:].rearrange("p (r t) -> p r t", t=64)[:, :, 0:63], op0=ALU.mult, op1=ALU.add),
                                reads=[t_PB[pa], t_fcwT, t_c1[u]], writes=[t_c1[u]])
                            S.act(lambda e, u=u: e.activation(out=gl[u][:], in_=c1[u][:], func=AF.Gelu_apprx_tanh),
                                  reads=[t_c1[u]], writes=[t_gl[u]])
                            S.pool(lambda e, u=u, ab_=ab_, jj=jj: e.tensor_tensor(
                                out=actT[ab_][:, jj, :], in0=gl[u][:], in1=bsb[u][:], op=ALU.mult),
                                reads=[t_gl[u], t_bsb[u]], writes=[t_actT[ab_]])
                        for tt in range(4):
                            ti = 4 * tb + tt
                            for hh in range(2):
                                pc = 4 + (tt * 2 + hh) % 2
                                for jj in range(G):
                                    S.pe(lambda e, ab_=ab_, jj=jj, tt=tt, hh=hh, wb=wb, pc=pc, G=G: e.matmul(
                                        bank(pc), actT[ab_][:, jj, 128 * tt:128 * tt + 128], wd[wb][:, jj, 512 * hh:512 * hh + 512],
                                        start=(jj == 0), stop=(jj == G - 1)),
                                        reads=[t_actT[ab_], t_wd[wb]], writes=[t_PB[pc]])
                                S.dve(lambda e, ti=ti, hh=hh, pc=pc: e.tensor_tensor(
                                    out=X1[:, ti, 512 * hh:512 * hh + 512], in0=bank(pc), in1=X1[:, ti, 512 * hh:512 * hh + 512],
                                    op=ALU.add), reads=[t_PB[pc], t_x1[ti]], writes=[t_x1[ti]])
                fjunk = sbp("fjunk", [128, D], BF16); t_fjunk = T("fjunk")
                for _i in range(int(os.environ.get("K_DUMMY", 0))):
                    S.dve(lambda e: e.memset(fjunk[:, 0:64], 0.0), writes=[t_fjunk])
                fo = [sbp("fo%d" % i, [128, D]) for i in range(1)] * 2
                t_fo = [T("fo0")] * 2
                fs4 = [sbp("fs4_%d" % i, [128, 4]) for i in range(2)]
                t_fs4 = [T("fs4_0"), T("fs4_1")]
                for i in range(NT if not int(os.environ.get("K_NOFIN", 0)) else 0):
                    b = i % 2
                    s4 = fs4[b]
                    S.act(lambda e, i=i, s4=s4: e.activation(out=fjunk[:], in_=X1[:, i, :], func=AF.Square, accum_out=s4[:, 0:1]),
                          reads=[t_x1[i]], writes=[t_fjunk, t_fs4[b]])
                    S.dve(lambda e, s4=s4: e.tensor_scalar(out=s4[:, 1:2], in0=s4[:, 0:1], scalar1=1.0 / D, scalar2=EPS,
                                                           op0=ALU.mult, op1=ALU.add), reads=[t_fs4[b]], writes=[t_fs4[b]])
                    S.act(lambda e, s4=s4: e.activation(out=s4[:, 2:3], in_=s4[:, 1:2], func=AF.Sqrt),
                          reads=[t_fs4[b]], writes=[t_fs4[b]])
                    S.dve(lambda e, s4=s4: e.reciprocal(out=s4[:, 3:4], in_=s4[:, 2:3]), reads=[t_fs4[b]], writes=[t_fs4[b]])
                    S.dve(lambda e, i=i, b=b, s4=s4: e.scalar_tensor_tensor(
                        out=fo[b][:], in0=X1[:, i, :], scalar=s4[:, 3:4], in1=gfR[:], op0=ALU.mult, op1=ALU.mult),
                        reads=[t_x1[i], t_fs4[b], t_gfR], writes=[t_fo[b]])
                    S.dma("sp", lambda e, i=i, b=b: e.dma_start(out=out[128 * i:128 * i + 128, :], in_=fo[b][:]),
                          reads=[t_fo[b]])
                S.barrier()

        if dbg:
            S.dma("sp", lambda e: e.dma_start(out=dbgo["hT"][:, :, 0:HALF], in_=hT[:]), reads=t_hT)
            S.dma("sp", lambda e: e.dma_start(out=dbgo["hT"][:, :, HALF:SEQ], in_=hTt), reads=t_hT)
            S.dma("sp", lambda e: e.dma_start(out=dbgo["hTc"], in_=hTc[:]), reads=t_hTc)
            S.dma("sp", lambda e: e.dma_start(out=dbgo["modT"], in_=modT[:]), reads=[t_modT])
            S.dma("sp", lambda e: e.dma_start(out=dbgo["gtR"], in_=gtR[:]), reads=[t_gtR])

        S.finish()
        with nc.Block() as block:
            S.emit(block)
    print("ops", S.nop, "waits", S.nwait, {e: len(S.prog[e]) for e in S.ALL}, "cnt", S.cnt, "dval", S.dval)
    return nc


def fm(v, n):
    return np.ascontiguousarray(np.asarray(v, np.float32).reshape(n, 128).T)


def make_in_maps(inp):
    maps = []
    x = np.asarray(inp["x"], np.float32)
    ctx = np.asarray(inp["ctx"], np.float32)
    c = np.asarray(inp["c"], np.float32)
    c_ctx = np.asarray(inp["c_ctx"], np.float32)
    w_mod = np.ascontiguousarray(np.asarray(inp["w_mod"], np.float32)[0])
    b_mod = np.asarray(inp["b_mod"], np.float32)[0]
    w_up = np.ascontiguousarray(np.asarray(inp["w_up"], np.float32)[0])
    w_down = np.ascontiguousarray(np.asarray(inp["w_down"], np.float32)[0])
    fcw = np.asarray(inp["ffn_conv_w"], np.float32)[0]
    tap = [0, 1, 2]
    w_in0 = np.ascontiguousarray(np.asarray(inp["w_in"], np.float32)[0])
    w_in1 = w_in0.copy()
    ba = w_in0[:, OFF_BA:OFF_BA + 32]
    w_in1[:, OFF_BA:OFF_BA + 32] = np.concatenate([ba[:, 8:16], ba[:, 0:8], ba[:, 24:32], ba[:, 16:24]], axis=1)
    cq = np.asarray(inp["conv_qkv"], np.float32)[0]
    alog = np.asarray(inp["a_log"], np.float32)[0]
    dtb = np.asarray(inp["dt_bias"], np.float32)[0]
    ong = np.asarray(inp["onorm_g"], np.float32)[0]
    wpa = np.ascontiguousarray(np.asarray(inp["w_proj_a"], np.float32)[0])
    wpb = np.ascontiguousarray(np.asarray(inp["w_proj_b"], np.float32)[0])
    wout = np.ascontiguousarray(np.asarray(inp["w_out"], np.float32)[0])
    lg = np.asarray(inp["a_ln_g"], np.float32)[0]
    lb = np.asarray(inp["a_ln_b"], np.float32)[0]
    abs_ = np.asarray(inp["a_bs"], np.float32)[0]
    aws = np.asarray(inp["a_ws"], np.float32)[0]
    for core in range(8):
        b, hf = core // 2, core % 2
        if hf == 0:
            xo, xt, cx = x[b, :HALF], x[b, HALF:], ctx[b]
        else:
            xf = x[b, ::-1]
            xo, xt, cx = xf[:HALF], xf[HALF:], ctx[b, ::-1]
        cvec = np.stack([fm(c[b], KC), fm(c_ctx, KC)], axis=-1)
        m = {
            "xo": np.ascontiguousarray(xo), "xt": np.ascontiguousarray(xt), "cx": np.ascontiguousarray(cx),
            "cvec": np.ascontiguousarray(cvec), "w_mod": w_mod, "b_modT": fm(b_mod, 48),
            "b_modR": np.ascontiguousarray(b_mod[None, :]),
            "g1T": fm(inp["norm1_g"][0], KC), "g2T": fm(inp["norm2_g"][0], KC),
            "gfR": np.ascontiguousarray(np.broadcast_to(np.asarray(inp["final_g"], np.float32)[None, :], (128, D))),
            "w_up": w_up, "w_down": w_down,
            "w_in": w_in0 if hf == 0 else w_in1,
            "w_proj_a": wpa, "w_proj_b": wpb, "w_out": wout,
            "lgR": np.ascontiguousarray(np.broadcast_to(lg[None, :], (128, D))),
            "lbrow": np.ascontiguousarray(lb[None, :]),
            "bsrow": np.ascontiguousarray((abs_ if hf == 0 else abs_[:, ::-1]).reshape(1, D)),
            "wsT": np.ascontiguousarray((aws if hf == 0 else aws[:, ::-1, ::-1]).transpose(2, 0, 1)),
            "cwT": np.ascontiguousarray(np.stack([fm(cq[t], 24) for t in (tap if hf == 0 else tap[::-1])], axis=-1)),
            "alogR": np.ascontiguousarray(np.broadcast_to((alog if hf == 0 else alog[::-1]).reshape(1, 16), (128, 16))),
            "dtbR": np.ascontiguousarray(np.broadcast_to((dtb if hf == 0 else dtb[::-1]).reshape(1, 16), (128, 16))),
            "ongT": np.ascontiguousarray(ong.reshape(128, 1)),
            "fcwT": np.ascontiguousarray(np.stack([fm(fcw[t], NFF) for t in (tap if hf == 0 else tap[::-1])], axis=-1)),
            "fcbT": fm(inp["ffn_conv_b"][0], NFF),
        }
        maps.append(m)
    return maps


_NC_CACHE = {}


def kernel(**inputs):
    if "full" not in _NC_CACHE:
        _NC_CACHE["full"] = build()
    nc = _NC_CACHE["full"]
    maps = make_in_maps(inputs)
    res = run_bass_kernel_spmd(nc, maps, core_ids=list(range(8)))
    outs = np.zeros((4, SEQ, D), np.float32)
    for core in range(8):
        b, hf = core // 2, core % 2
        o = res.results[core]["out"]
        if hf == 0:
            outs[b, :HALF] = o
        else:
            outs[b, HALF:] = o[::-1]
    return outs
```

```python
import numpy as np
import os
from contextlib import ExitStack
import concourse.bass as bass
import concourse.mybir as mybir
from concourse.bass_utils import run_bass_kernel_spmd

F32 = mybir.dt.float32
BF16 = mybir.dt.bfloat16
F32R = mybir.dt.float32r
AF = mybir.ActivationFunctionType
ALU = mybir.AluOpType

D = 1024
KC = 8
SEQ = 4096
HALF = 2048
NT = 16
CTX = 256
EPS = 1e-6
IN_COLS = 8224
OFF_K, OFF_V, OFF_BA, OFF_Q, OFF_Z, OFF_U, OFF_AV, OFF_GA, OFF_GB = 0, 1024, 2048, 2080, 3104, 4128, 5152, 6176, 7200
DFF = 2816
NFF = 22


class T:
    __slots__ = ("name", "w", "r", "x")

    def __init__(self, name, x=False):
        self.name = name
        self.w = None
        self.r = {}
        self.x = x


class Sched:
    CE = ("pe", "act", "dve", "pool")
    ALL = ("pe", "act", "dve", "pool", "sp")

    def __init__(self, nc, stack, n_dma=40):
        self.nc = nc
        self.sem = {e: stack.enter_context(nc.semaphore("s_" + e)) for e in self.CE}
        self.cnt = {e: 0 for e in self.CE}
        self.prog = {e: [] for e in self.ALL}
        self.clock = {e: {} for e in self.ALL}
        self.snap = {}
        self.dsem = [stack.enter_context(nc.semaphore("d%d" % i)) for i in range(n_dma)]
        self.dval = [0] * n_dma
        self.dpool = {"sp": list(range(0, n_dma // 2)), "pool": list(range(n_dma // 2, n_dma * 3 // 4)),
                      "act": list(range(n_dma * 3 // 4, n_dma))}
        self.dnext = {"sp": 0, "pool": 0, "act": 0}
        self.nwait = 0
        self.nop = 0

    def _semof(self, key):
        if isinstance(key, tuple):
            return self.dsem[key[1]]
        return self.sem[key]

    def need(self, eng, ev):
        key, val = ev
        ck = self.clock[eng]
        if ck.get(key, 0) >= val:
            return
        self.prog[eng].append(("wait", self._semof(key), val))
        self.nwait += 1
        sn = self.snap.get(ev)
        if sn:
            for k, v in sn.items():
                if ck.get(k, 0) < v:
                    ck[k] = v
        ck[key] = val

    def _deps(self, eng, reads, writes):
        deps = set()
        for t in reads:
            if t.w is not None:
                deps.add(t.w)
            if t.x:
                for rk, ev in t.r.items():
                    if rk != eng:
                        deps.add(ev)
        for t in writes:
            if t.w is not None:
                deps.add(t.w)
            for ev in t.r.values():
                deps.add(ev)
        if eng == "pe":
            deps = {d for d in deps if d[0] != "pe"}
        for ev in sorted(deps, key=lambda x: (str(x[0]), x[1])):
            self.need(eng, ev)

    def _mark(self, rkey, ev, reads, writes):
        for t in reads:
            t.r[rkey] = ev
        for t in writes:
            t.w = ev
            t.r = {}

    def op(self, eng, fn, reads=(), writes=()):
        self._deps(eng, reads, writes)
        self.cnt[eng] += 1
        ev = (eng, self.cnt[eng])
        self.prog[eng].append(("op", fn, self.sem[eng], 1))
        self.snap[ev] = dict(self.clock[eng])
        self._mark(eng, ev, reads, writes)
        self.nop += 1
        return ev

    def pe(self, fn, reads=(), writes=()):
        return self.op("pe", fn, reads, writes)

    def act(self, fn, reads=(), writes=()):
        return self.op("act", fn, reads, writes)

    def dve(self, fn, reads=(), writes=()):
        return self.op("dve", fn, reads, writes)

    def pool(self, fn, reads=(), writes=()):
        return self.op("pool", fn, reads, writes)

    def dma(self, q, fn, reads=(), writes=()):
        self._deps(q, reads, writes)
        lst = self.dpool[q]
        i = lst[self.dnext[q] % len(lst)]
        self.dnext[q] += 1
        key = ("d", i)
        if self.dval[i] > 0:
            self.need(q, (key, self.dval[i]))
        self.dval[i] += 16
        ev = (key, self.dval[i])
        self.prog[q].append(("op", fn, self.dsem[i], 16))
        self.snap[ev] = dict(self.clock[q])
        self._mark(key, ev, reads, writes)
        return ev

    def _all_events(self):
        evs = [(e, self.cnt[e]) for e in self.CE if self.cnt[e] > 0]
        evs += [(("d", i), v) for i, v in enumerate(self.dval) if v > 0]
        return evs

    def barrier(self):
        evs = self._all_events()
        for e in self.ALL:
            for ev in evs:
                self.need(e, ev)

    def finish(self):
        for ev in self._all_events():
            self.need("sp", ev)

    def emit(self, block):
        prog = self.prog

        def run(h, lst):
            for ent in lst:
                if ent[0] == "wait":
                    h.wait_ge(ent[1], ent[2])
                else:
                    ins = ent[1](h)
                    ins.then_inc(ent[2], ent[3])

        @block.sync
        def _(h):
            run(h, prog["sp"])

        @block.tensor
        def _(h):
            run(h, prog["pe"])

        @block.scalar
        def _(h):
            run(h, prog["act"])

        @block.vector
        def _(h):
            run(h, prog["dve"])

        @block.gpsimd
        def _(h):
            run(h, prog["pool"])


class K:
    def __init__(self, nc, stack):
        self.nc = nc
        self.st = stack
        self.S = Sched(nc, stack)
        self.dbg_outs = []

    def sb(self, name, shape, dt=F32):
        return self.st.enter_context(self.nc.sbuf_tensor(un("s_" + name), list(shape), dt))

    def dram_in(self, name, shape, dt=F32):
        return self.nc.dram_tensor(name, list(shape), dt, kind="ExternalInput").ap()

    def dram_out(self, name, shape, dt=F32):
        return self.nc.dram_tensor(name, list(shape), dt, kind="ExternalOutput").ap()


_UNIQ = [0]


def un(name):
    _UNIQ[0] += 1
    return "%s_%d" % (name, _UNIQ[0])


def build(stage="full", dbg=False):
    nc = bass.Bass("TRN2", target_bir_lowering=False)
    with ExitStack() as st:
        k = K(nc, st)
        S = k.S
        I = {}
        for name, shape in [
            ("xo", [HALF, D]), ("xt", [HALF, D]), ("cx", [CTX, D]), ("cvec", [128, KC, 2]),
            ("w_mod", [D, 6 * D]), ("b_modT", [128, 48]), ("b_modR", [1, 6 * D]),
            ("g1T", [128, KC]), ("g2T", [128, KC]), ("gfR", [128, D]),
            ("w_in", [D, IN_COLS]), ("cwT", [128, 24, 3]), ("alogR", [128, 16]), ("dtbR", [128, 16]), ("ongT", [128, 1]),
            ("w_proj_a", [D, D]), ("w_proj_b", [D, D]), ("w_out", [D, D]), ("lgR", [128, D]), ("lbrow", [1, D]),
            ("bsrow", [1, D]), ("wsT", [128, 8, 128]),
            ("w_up", [D, 2 * DFF]), ("fcwT", [128, NFF, 3]), ("fcbT", [128, NFF]), ("w_down", [DFF, D]),
        ]:
            I[name] = k.dram_in(name, shape)
        out = k.dram_out("out", [HALF, D])
        dbgo = {}
        if dbg:
            dbgo["hT"] = k.dram_out("d_hT", [128, KC, SEQ], BF16)
            dbgo["hTc"] = k.dram_out("d_hTc", [128, KC, CTX], BF16)
            dbgo["modT"] = k.dram_out("d_modT", [128, 48, 2])
            dbgo["gtR"] = k.dram_out("d_gtR", [128, 2, D])
            dbgo["kn"] = k.dram_out("d_kn", [128, SEQ + CTX])
            dbgo["vs"] = k.dram_out("d_vs", [128, SEQ + CTX])
            dbgo["qn"] = k.dram_out("d_qn", [128, HALF])
            dbgo["oT"] = k.dram_out("d_oT", [128, HALF])
            dbgo["beta"] = k.dram_out("d_beta", [128, 34, 16])
            dbgo["gc"] = k.dram_out("d_gc", [128, 34, 16])
            dbgo["Sst"] = k.dram_out("d_Sst", [128, 128])

        ident = k.sb("ident", [128, 128]); t_ident = T("ident")
        ones = k.sb("ones", [128, 128]); t_ones = T("ones")
        hT = k.sb("hT", [128, KC, HALF], BF16)
        t_hT = [T("hT%d" % i) for i in range(32)]
        hTc = k.sb("hTc", [128, KC, CTX], BF16)
        t_hTc = [T("hTc%d" % i) for i in range(2)]
        modT = k.sb("modT", [128, 48, 2]); t_modT = T("modT")
        gtR = k.sb("gtR", [128, 2, D]); t_gtR = T("gtR")
        a1T = k.sb("a1T", [128, KC, 2]); t_a1T = T("a1T")
        a2T = k.sb("a2T", [128, KC, 2]); t_a2T = T("a2T")
        g1T = k.sb("g1T", [128, KC]); t_g1T = T("g1T")
        g2T = k.sb("g2T", [128, KC]); t_g2T = T("g2T")

        X1 = k.sb("X1", [128, NT, D]); t_x1 = [T("x1_%d" % i) for i in range(NT)]
        hTt = X1[:, 0:8, :].bitcast(BF16)

        PP = [st.enter_context(nc.psum_tensor("pp%d" % i, [128, 1024], F32)) for i in range(4)]
        t_PB = [T("pb%d" % i, x=True) for i in range(8)]

        def bank(i):
            return PP[i // 2][:, (i % 2) * 512:(i % 2) * 512 + 512]

        _cast_rr = [0]

        def load_cast(stg, dst, t_dst, src, n):
            (sa, t_sa) = stg[_cast_rr[0] % len(stg)]
            eng = ("act", "dve", "pool")[_cast_rr[0] % 3]
            _cast_rr[0] += 1
            sv = sa[:, 0:n]
            if len(dst.shape) == 3:
                sv = sv.rearrange("p (a b) -> p a b", b=dst.shape[2])
            S.dma("sp", lambda e: e.dma_start(out=sv, in_=src), writes=[t_sa])
            if eng == "act":
                S.act(lambda e: e.activation(out=dst, in_=sv, func=AF.Identity), [t_sa], [t_dst])
            else:
                S.op(eng, lambda e: e.tensor_copy(out=dst, in_=sv), [t_sa], [t_dst])

        S.pool(lambda e: e.memset(ones[:], 1.0), writes=[t_ones])
        S.pool(lambda e: e.affine_select(out=ident[:], in_=ones[:], pattern=[[1, 128]], compare_op=ALU.is_equal,
                                         fill=0.0, base=0, channel_multiplier=-1),
               reads=[t_ones], writes=[t_ident])
        triU = k.sb("triU", [128, 128]); triL = k.sb("triL", [128, 128])
        strU = k.sb("strU", [128, 128]); strL = k.sb("strL", [128, 128])
        t_msk = T("masks")
        S.pool(lambda e: e.affine_select(out=triU[:], in_=ones[:], pattern=[[1, 128]], compare_op=ALU.is_ge,
                                         fill=0.0, base=0, channel_multiplier=-1), reads=[t_ones], writes=[t_msk])
        S.pool(lambda e: e.affine_select(out=strU[:], in_=ones[:], pattern=[[1, 128]], compare_op=ALU.is_gt,
                                         fill=0.0, base=0, channel_multiplier=-1), reads=[t_ones], writes=[t_msk])
        S.pool(lambda e: e.affine_select(out=triL[:], in_=ones[:], pattern=[[-1, 128]], compare_op=ALU.is_ge,
                                         fill=0.0, base=0, channel_multiplier=1), reads=[t_ones], writes=[t_msk])
        S.pool(lambda e: e.affine_select(out=strL[:], in_=ones[:], pattern=[[-1, 128]], compare_op=ALU.is_gt,
                                         fill=0.0, base=0, channel_multiplier=1), reads=[t_ones], writes=[t_msk])
        S.pool(lambda e: e.memset(modT[:], 0.0), writes=[t_modT])
        S.dve(lambda e: e.tensor_scalar(out=strU[:], in0=strU[:], scalar1=-1.0, scalar2=None, op0=ALU.mult), [t_msk], [t_msk])
        S.dve(lambda e: e.tensor_scalar(out=strL[:], in0=strL[:], scalar1=-1.0, scalar2=None, op0=ALU.mult), [t_msk], [t_msk])
        S.dma("sp", lambda e: e.dma_start(out=g1T[:], in_=I["g1T"]), writes=[t_g1T])
        S.dma("sp", lambda e: e.dma_start(out=g2T[:], in_=I["g2T"]), writes=[t_g2T])

        with ExitStack() as ph:
            def sbp(name, shape, dt=F32):
                return ph.enter_context(nc.sbuf_tensor(un("s_" + name), list(shape), dt))
            cv = sbp("cv", [128, KC, 2]); t_cv = T("cv")
            cond = sbp("cond", [128, KC, 2]); t_cond = T("cond")
            condB = sbp("condB", [128, KC, 128]); t_condB = T("condB")
            bmT = sbp("bmT", [128, 48]); t_bmT = T("bmT")
            bmR = sbp("bmR", [1, 6 * D]); t_bmR = T("bmR")
            wm = [sbp("wm%d" % i, [128, KC, 512]) for i in range(2)]
            t_wm = [T("wm0"), T("wm1")]
            S.dma("sp", lambda e: e.dma_start(out=cv[:], in_=I["cvec"]), writes=[t_cv])
            S.dma("sp", lambda e: e.dma_start(out=bmT[:], in_=I["b_modT"]), writes=[t_bmT])
            S.dma("sp", lambda e: e.dma_start(out=bmR[:], in_=I["b_modR"]), writes=[t_bmR])
            S.act(lambda e: e.activation(out=cond[:], in_=cv[:], func=AF.Silu), reads=[t_cv], writes=[t_cond])
            for kc in range(KC):
                S.dve(lambda e, kc=kc: e.tensor_scalar(out=condB[:, kc, :], in0=ones[:], scalar1=cond[:, kc, 0:1],
                                                       scalar2=None, op0=ALU.mult),
                      reads=[t_ones, t_cond], writes=[t_condB])
            wsrc = I["w_mod"].rearrange("(kc p) n -> p kc n", p=128)
            FMJ = {0: 0, 1: 4, 2: 8, 3: 12, 6: 24, 7: 28, 8: 32, 9: 36}
            RBJ = {4: (0, 0), 5: (0, 1), 10: (1, 0), 11: (1, 1)}
            for j in range(12):
                b = j % 2
                S.dma("sp", lambda e, j=j, b=b: e.dma_start(out=wm[b][:], in_=wsrc[:, :, 512 * j:512 * j + 512]),
                      writes=[t_wm[b]])
                pb = 0 + (j % 2)
                if j in FMJ:
                    for ct in range(4):
                        for kc in range(KC):
                            S.pe(lambda e, ct=ct, kc=kc, b=b, pb=pb: e.matmul(
                                bank(pb)[:, 2 * ct:2 * ct + 2], wm[b][:, kc, 128 * ct:128 * ct + 128], cond[:, kc, :],
                                start=(kc == 0), stop=(kc == KC - 1)),
                                reads=[t_wm[b], t_cond], writes=[t_PB[pb]])
                    j0 = FMJ[j]
                    S.dve(lambda e, pb=pb, j0=j0: e.tensor_tensor(
                        out=modT[:, j0:j0 + 4, :], in0=bank(pb)[:, 0:8].rearrange("p (c m) -> p c m", m=2),
                        in1=bmT[:, j0:j0 + 4].unsqueeze(2).broadcast_to([128, 4, 2]), op=ALU.add),
                        reads=[t_PB[pb], t_bmT], writes=[t_modT])
                else:
                    which, half = RBJ[j]
                    for kc in range(KC):
                        S.pe(lambda e, kc=kc, b=b, pb=pb: e.matmul(
                            bank(pb), condB[:, kc, :], wm[b][:, kc, :], start=(kc == 0), stop=False),
                            reads=[t_wm[b], t_condB], writes=[t_PB[pb]])
                    S.pe(lambda e, j=j, pb=pb: e.matmul(
                        bank(pb), ones[0:1, :], bmR[0:1, 512 * j:512 * j + 512], start=False, stop=True),
                        reads=[t_ones, t_bmR], writes=[t_PB[pb]])
                    S.act(lambda e, pb=pb, which=which, half=half: e.activation(
                        out=gtR[:, which, 512 * half:512 * half + 512], in_=bank(pb), func=AF.Identity),
                        reads=[t_PB[pb]], writes=[t_gtR])
            for (aT, t_aT, gT, t_gT, j0) in ((a1T, t_a1T, g1T, t_g1T, 8), (a2T, t_a2T, g2T, t_g2T, 32)):
                S.dve(lambda e, aT=aT, gT=gT, j0=j0: e.scalar_tensor_tensor(
                    out=aT[:], in0=modT[:, j0:j0 + 8, :], scalar=1.0,
                    in1=gT[:].unsqueeze(2).broadcast_to([128, KC, 2]), op0=ALU.add, op1=ALU.mult),
                    reads=[t_modT, t_gT], writes=[t_aT])
            S.barrier()

        def norm_tiles(ph, jobs, aT, t_aT, sh_j0, pbase):
            xs = [ph.enter_context(nc.sbuf_tensor(un("xs%d" % i), [128, D], F32)) for i in range(2)]
            t_xs = [T("xs0"), T("xs1")]
            xn = [ph.enter_context(nc.sbuf_tensor(un("xn%d" % i), [128, D], F32)) for i in range(2)]
            t_xn = [T("xn0"), T("xn1")]
            junk = ph.enter_context(nc.sbuf_tensor(un("junk"), [128, D], BF16)); t_junk = T("junk")
            tmp = [ph.enter_context(nc.sbuf_tensor(un("ntmp%d" % i), [128, KC, 128], F32)) for i in range(2)]
            t_tmp = [T("ntmp0"), T("ntmp1")]
            st4 = [ph.enter_context(nc.sbuf_tensor(un("nst%d" % i), [128, 4], F32)) for i in range(2)]
            t_st4 = [T("nst0"), T("nst1")]
            for n, (src, dst, t_dst, m) in enumerate(jobs):
                b = n % 2
                if isinstance(src, tuple):
                    xin, t_xin = src
                else:
                    S.dma("sp", lambda e, src=src, b=b: e.dma_start(out=xs[b][:], in_=src), writes=[t_xs[b]])
                    xin, t_xin = xs[b][:], t_xs[b]
                s4 = st4[b]
                S.act(lambda e, xin=xin, s4=s4: e.activation(out=junk[:], in_=xin, func=AF.Square, accum_out=s4[:, 0:1]),
                      reads=[t_xin], writes=[t_junk, t_st4[b]])
                S.dve(lambda e, s4=s4: e.tensor_scalar(out=s4[:, 1:2], in0=s4[:, 0:1], scalar1=1.0 / D, scalar2=EPS,
                                                       op0=ALU.mult, op1=ALU.add), reads=[t_st4[b]], writes=[t_st4[b]])
                S.act(lambda e, s4=s4: e.activation(out=s4[:, 2:3], in_=s4[:, 1:2], func=AF.Sqrt),
                      reads=[t_st4[b]], writes=[t_st4[b]])
                S.dve(lambda e, s4=s4: e.reciprocal(out=s4[:, 3:4], in_=s4[:, 2:3]), reads=[t_st4[b]], writes=[t_st4[b]])
                S.act(lambda e, xin=xin, s4=s4, b=b: e.activation(out=xn[b][:], in_=xin, func=AF.Identity, scale=s4[:, 3:4]),
                      reads=[t_xin, t_st4[b]], writes=[t_xn[b]])
                pp = pbase + b
                for kc in range(KC):
                    S.pe(lambda e, kc=kc, b=b, pp=pp: e.transpose(out=PP[pp][:, 128 * kc:128 * kc + 128],
                                                                  in_=xn[b][:, 128 * kc:128 * kc + 128], identity=ident[:]),
                         reads=[t_xn[b], t_ident], writes=[t_PB[2 * pp + kc // 4]])
                S.dve(lambda e, b=b, pp=pp, m=m: e.tensor_tensor(
                    out=tmp[b][:], in0=PP[pp][:].rearrange("p (c t) -> p c t", t=128),
                    in1=aT[:, :, m:m + 1].broadcast_to([128, KC, 128]), op=ALU.mult),
                    reads=[t_PB[2 * pp], t_PB[2 * pp + 1], t_aT], writes=[t_tmp[b]])
                S.pool(lambda e, b=b, dst=dst, m=m: e.tensor_tensor(
                    out=dst, in0=tmp[b][:], in1=modT[:, sh_j0:sh_j0 + 8, m:m + 1].broadcast_to([128, KC, 128]), op=ALU.add),
                    reads=[t_tmp[b], t_modT], writes=[t_dst])

        with ExitStack() as ph:
            jobs = []
            for i in range(2):
                jobs.append((I["cx"][128 * i:128 * i + 128, :], hTc[:, :, 128 * i:128 * i + 128], t_hTc[i], 1))
            for i in range(16 if stage != "ffn_only" else 0):
                jobs.append((I["xt"][128 * i:128 * i + 128, :], hTt[:, :, 128 * i:128 * i + 128], t_hT[16 + i], 0))
            for i in range(16):
                jobs.append((I["xo"][128 * i:128 * i + 128, :], hT[:, :, 128 * i:128 * i + 128], t_hT[i], 0))
            norm_tiles(ph, jobs, a1T, t_a1T, 0, 1)
            S.barrier()

        OGD = None
        if stage in ("full", "delta"):
            OGD = k.dram_out("d_og", [128, 8, HALF], BF16) if dbg else nc.dram_tensor("ogd_scratch", [128, 8, HALF], BF16).ap()
        if stage in ("full", "delta"):
            with ExitStack() as ph:
                def sbp(name, shape, dt=F32):
                    return ph.enter_context(nc.sbuf_tensor(un("s_" + name), list(shape), dt))
                _rr = {}

                def ring(name, n, shape, dt=F32):
                    tiles = [(sbp("%s%d" % (name, i), shape, dt), T("%s%d" % (name, i))) for i in range(n)]
                    _rr[name] = [tiles, 0]

                def nxt(name):
                    tiles, i = _rr[name]
                    _rr[name][1] = i + 1
                    return tiles[i % len(tiles)]
                _pb = [0]

                def nbank():
                    i = _pb[0] % 8
                    _pb[0] += 1
                    return bank(i), t_PB[i]

                def mm(o, lhsT, rhs, r, w, start=True, stop=True):
                    S.pe(lambda e: e.matmul(o, lhsT, rhs, start=start, stop=stop), reads=r, writes=w)

                def trp(o, in_, r, w):
                    S.pe(lambda e: e.transpose(out=o, in_=in_, identity=ident[:]), reads=list(r) + [t_ident], writes=w)

                def tt(eng, o, a, b, op, r, w):
                    S.op(eng, lambda e: e.tensor_tensor(out=o, in0=a, in1=b, op=op), r, w)

                def actf(o, in_, func, r, w, scale=1.0, bias=None):
                    if bias is None:
                        S.act(lambda e: e.activation(out=o, in_=in_, func=func, scale=scale), r, w)
                    else:
                        S.act(lambda e: e.activation(out=o, in_=in_, func=func, scale=scale, bias=bias), r, w)

                cwT = sbp("cwT", [128, 24, 3]); t_cwT = T("cwT")
                alogR = sbp("alogR", [128, 16]); dtbR = sbp("dtbR", [128, 16]); negA = sbp("negA", [128, 16])
                ongT = sbp("ongT", [128, 1]); t_cst = T("dcst")
                S.dma("sp", lambda e: e.dma_start(out=cwT[:], in_=I["cwT"]), writes=[t_cwT])
                S.dma("sp", lambda e: e.dma_start(out=alogR[:], in_=I["alogR"]), writes=[t_cst])
                S.dma("sp", lambda e: e.dma_start(out=dtbR[:], in_=I["dtbR"]), writes=[t_cst])
                S.dma("sp", lambda e: e.dma_start(out=ongT[:], in_=I["ongT"]), writes=[t_cst])
                actf(negA[:], alogR[:], AF.Exp, [t_cst], [t_cst])
                S.dve(lambda e: e.tensor_scalar(out=negA[:], in0=negA[:], scalar1=-1.0, scalar2=None, op0=ALU.mult),
                      [t_cst], [t_cst])
                NTL = 34
                names = ["BETA", "GC", "EKD", "EGEND", "BSC"]
                SC = {n: sbp("sc_" + n, [128, NTL, 16]) for n in names}

                t_SC = T("SC")
                wba = sbp("wba", [128, KC, 32], BF16); t_wba = T("wba")
                winsrc = I["w_in"].rearrange("(kc p) n -> p kc n", p=128)
                stgD = [(sbp("stgD%d" % i, [128, 1024]), T("stgD%d" % i)) for i in range(1)]
                load_cast(stgD, wba[:], t_wba, winsrc[:, :, OFF_BA:OFF_BA + 32], KC * 32)
                ring("sct", 2, [128, 16])
                sctmp = ExitStack()
                for n in ["GEND", "EG", "GG"]:
                    SC[n] = sctmp.enter_context(nc.sbuf_tensor(un("s_sc_" + n), [128, NTL, 16], F32))

                def h_tile(ti):
                    if ti < 16:
                        return hT[:, :, 128 * ti:128 * ti + 128], t_hT[ti]
                    if ti < 32:
                        return hTt[:, :, 128 * (ti - 16):128 * (ti - 16) + 128], t_hT[ti]
                    return hTc[:, :, 128 * (ti - 32):128 * (ti - 32) + 128], t_hTc[ti - 32]
                for ti in range(NTL):
                    ha, t_ha = h_tile(ti)
                    pb, t_pb = nbank()
                    for kc in range(KC):
                        mm(pb[:, 0:32], ha[:, kc, :], wba[:, kc, :], [t_ha, t_wba], [t_pb], start=(kc == 0), stop=(kc == KC - 1))
                    actf(SC["BETA"][:, ti, :], pb[:, 0:16], AF.Sigmoid, [t_pb], [t_SC])
                    tmp, t_tmp = nxt("sct")
                    tt("dve", tmp[:], pb[:, 16:32], dtbR[:], ALU.add, [t_pb, t_cst], [t_tmp])
                    actf(tmp[:], tmp[:], AF.Exp, [t_tmp], [t_tmp])
                    actf(tmp[:], tmp[:], AF.Ln, [t_tmp], [t_tmp], bias=1.0)
                    tt("dve", SC["GG"][:, ti, :], tmp[:], negA[:], ALU.mult, [t_tmp, t_cst], [t_SC])
                    pc, t_pc = nbank()
                    mm(pc[:, 0:8], triU[:], SC["GG"][:, ti, 0:8], [t_msk, t_SC], [t_pc])
                    mm(pc[:, 8:16], triL[:], SC["GG"][:, ti, 8:16], [t_msk, t_SC], [t_pc])
                    mm(pc[:, 16:32], ones[:], SC["GG"][:, ti, :], [t_ones, t_SC], [t_pc])
                    actf(SC["GC"][:, ti, :], pc[:, 0:16], AF.Identity, [t_pc], [t_SC])
                    actf(SC["GEND"][:, ti, :], pc[:, 16:32], AF.Identity, [t_pc], [t_SC])
                actf(SC["EG"][:], SC["GC"][:], AF.Exp, [t_SC], [t_SC])
                actf(SC["EGEND"][:], SC["GEND"][:], AF.Exp, [t_SC], [t_SC])
                tt("dve", SC["EKD"][:], SC["GEND"][:], SC["GC"][:], ALU.subtract, [t_SC], [t_SC])
                actf(SC["EKD"][:], SC["EKD"][:], AF.Exp, [t_SC], [t_SC])
                tt("dve", SC["BSC"][:], SC["BETA"][:], SC["EG"][:], ALU.mult, [t_SC], [t_SC])
                S.barrier()
                sctmp.close()

                LK = SEQ + CTX
                kn = sbp("kn", [128, LK]); t_kn = T("kn")
                vs = sbp("vs", [128, LK]); t_vs = T("vs")
                raw = X1[:, 8:12, :].rearrange("p a n -> p (a n)"); t_raw = T("raw")
                qn = X1[:, 12:14, :].rearrange("p a n -> p (a n)"); t_qn = T("qn")
                oT = X1[:, 14:16, :].rearrange("p a n -> p (a n)"); t_oT = [T("oT%d" % i) for i in range(NT)]
                rawc = sbp("rawc", [128, CTX]); t_rawc = T("rawc")
                wh = sbp("wh", [128, KC, 4, 128], BF16); t_wh = T("wh")
                ogs = sbp("ogs", [128, HALF], BF16); t_ogs = T("ogs")
                ring("blk", 3, [128, 512])

                def project(ct, dst, t_dst, segs):
                    for (ha, t_has, c0, n) in segs:
                        pb, t_pb = nbank()
                        for kc in range(KC):
                            mm(pb[:, 0:n], wh[:, kc, ct, :], ha[:, kc, :], [t_wh] + list(t_has), [t_pb],
                               start=(kc == 0), stop=(kc == KC - 1))
                        actf(dst[:, c0:c0 + n], pb[:, 0:n], AF.Identity, [t_pb], [t_dst])

                def conv_silu(src, t_src, dst, t_dst, L_in, L_out, cwi):
                    actf(dst[:, 0:L_out], src[:, 0:L_out], AF.Identity, [t_src, t_cwT], [t_dst], scale=cwT[:, cwi, 1:2])
                    S.dve(lambda e: e.scalar_tensor_tensor(out=dst[:, 1:L_out], in0=src[:, 0:L_out - 1], scalar=cwT[:, cwi, 0:1],
                                                           in1=dst[:, 1:L_out], op0=ALU.mult, op1=ALU.add),
                          [t_src, t_cwT, t_dst], [t_dst])
                    hi = min(L_out, L_in - 1)
                    S.dve(lambda e: e.scalar_tensor_tensor(out=dst[:, 0:hi], in0=src[:, 1:hi + 1], scalar=cwT[:, cwi, 2:3],
                                                           in1=dst[:, 0:hi], op0=ALU.mult, op1=ALU.add),
                          [t_src, t_cwT, t_dst], [t_dst])
                    actf(dst[:, 0:L_out], dst[:, 0:L_out], AF.Silu, [t_dst], [t_dst])

                def l2n(buf, t_buf, c0, n, mult):
                    sq, t_sq = nxt("blk")
                    actf(sq[:, 0:n], buf[:, c0:c0 + n], AF.Square, [t_buf], [t_sq])
                    pb, t_pb = nbank()
                    mm(pb[:, 0:n], ones[:], sq[:, 0:n], [t_ones, t_sq], [t_pb])
                    r1, t_r1 = nxt("blk")
                    S.dve(lambda e: e.tensor_scalar(out=r1[:, 0:n], in0=pb[:, 0:n], scalar1=EPS, scalar2=None, op0=ALU.add),
                          [t_pb], [t_r1])
                    actf(r1[:, 0:n], r1[:, 0:n], AF.Ln, [t_r1], [t_r1])
                    actf(r1[:, 0:n], r1[:, 0:n], AF.Exp, [t_r1], [t_r1], scale=-0.5)
                    S.dve(lambda e: e.scalar_tensor_tensor(out=buf[:, c0:c0 + n], in0=buf[:, c0:c0 + n], scalar=float(mult),
                                                           in1=r1[:, 0:n], op0=ALU.mult, op1=ALU.mult),
                          [t_buf, t_r1], [t_buf])

                seg_own = [(hT[:, :, 512 * b:512 * b + 512], t_hT[4 * b:4 * b + 4], 512 * b, 512) for b in range(4)]
                seg_oth = [(hTt[:, :, 512 * b:512 * b + 512], t_hT[16 + 4 * b:16 + 4 * b + 4], HALF + 512 * b, 512) for b in range(4)]
                seg_ctx = [(hTc[:, :, :], t_hTc, 0, CTX)]
                seg_q2 = [(hTt[:, :, 0:2], t_hT[16:17], HALF, 2)]

                WU = int(os.environ.get("K_WU", 3))
                for sl in range(WU):
                    ring("m%d" % sl, 9, [128, 128])
                    ring("t%d" % sl, 8, [128, 128])
                Sd = [(sbp("Sst%d" % d, [128, 128]), T("S%d" % d)) for d in range(2)]

                def unit(h, d, ti, full, sl):
                    M_, T_ = "m%d" % sl, "t%d" % sl
                    Sst, t_S = Sd[d]
                    col = d * 8 + h
                    t0 = 128 * ti if ti < 32 else SEQ + 128 * (ti - 32)
                    kTc = kn[:, t0:t0 + 128]
                    vTc = vs[:, t0:t0 + 128]
                    Tri_, inclT_, strict_ = (triU, triU, strL) if d == 0 else (triL, triL, strU)
                    sc = lambda n: SC[n][:, ti, col:col + 1]
                    pa, t_pa = nbank()
                    trp(pa[:, 0:128], kTc, [t_kn], [t_pa])
                    kbg, t_kbg = nxt(M_)
                    actf(kbg[:], pa[:, 0:128], AF.Identity, [t_pa, t_SC], [t_kbg], scale=sc("BSC"))
                    kd, t_kd = nxt(M_)
                    S.dve(lambda e: e.tensor_scalar(out=kd[:], in0=pa[:, 0:128], scalar1=sc("EKD"), scalar2=None, op0=ALU.mult),
                          [t_pa, t_SC], [t_kd])
                    yield 0
                    pv, t_pv = nbank()
                    trp(pv[:, 0:128], vTc, [t_vs], [t_pv])
                    vb, t_vb = nxt(M_)
                    actf(vb[:], pv[:, 0:128], AF.Identity, [t_pv, t_SC], [t_vb], scale=sc("BETA"))
                    dg, t_dg = nxt(T_)
                    S.dve(lambda e: e.tensor_scalar(out=dg[:], in0=ident[:], scalar1=sc("GC"), scalar2=None, op0=ALU.mult),
                          [t_ident, t_SC], [t_dg])
                    yield 0
                    pr, t_pr = nbank()
                    mm(pr[:, 0:128], ones[:], dg[:], [t_ones, t_dg], [t_pr])
                    E, t_E = nxt(T_)
                    actf(E[:], pr[:, 0:128], AF.Abs, [t_pr, t_SC], [t_E], scale=-1.0, bias=sc("GC"))
                    if full:
                        egr, t_egr = nxt(T_)
                        actf(egr[:], pr[:, 0:128], AF.Exp, [t_pr], [t_egr])
                    actf(E[:], E[:], AF.Exp, [t_E], [t_E], scale=-1.0)
                    yield 0
                    Es, t_Es = nxt(T_)
                    tt("pool", Es[:], E[:], strict_[:], ALU.mult, [t_E, t_msk], [t_Es])
                    pk, t_pk = nbank()
                    mm(pk[:, 0:128], kTc, kTc, [t_kn], [t_pk])
                    Nm, t_N = nxt(M_)
                    S.dve(lambda e, Nm=Nm: e.scalar_tensor_tensor(out=Nm[:], in0=pk[:, 0:128], scalar=sc("BETA"), in1=Es[:],
                                                                  op0=ALU.mult, op1=ALU.mult), [t_pk, t_SC, t_Es], [t_N])
                    yield 0
                    if full:
                        qTc = qn[:, 128 * ti:128 * ti + 128]
                        qg, t_qg = nxt(M_)
                        tt("pool", qg[:], qTc, egr[:], ALU.mult, [t_qn, t_egr], [t_qg])
                        Ei, t_Ei = nxt(T_)
                        tt("pool", Ei[:], E[:], inclT_[:], ALU.mult, [t_E, t_msk], [t_Ei])
                        pq, t_pq = nbank()
                        mm(pq[:, 0:128], kTc, qTc, [t_kn, t_qn], [t_pq])
                        qk, t_qk = nxt(M_)
                        tt("dve", qk[:], pq[:, 0:128], Ei[:], ALU.mult, [t_pq, t_Ei], [t_qk])
                        yield 0
                    pt, t_pt = nbank()
                    trp(pt[:, 0:128], Nm[:], [t_N], [t_pt])
                    Nt, t_Nt = nxt(T_)
                    actf(Nt[:], pt[:, 0:128], AF.Identity, [t_pt], [t_Nt])
                    Xt, t_Xt = nxt(T_)
                    tt("dve", Xt[:], pt[:, 0:128], ident[:], ALU.add, [t_pt, t_ident], [t_Xt])
                    yield 0
                    for lvl in range(6):
                        p1, t_p1 = nbank()
                        mm(p1[:, 0:128], Nt[:], Nm[:], [t_Nt, t_N], [t_p1])
                        if lvl < 5:
                            p2, t_p2 = nbank()
                            mm(p2[:, 0:128], Nm[:], Nt[:], [t_Nt, t_N], [t_p2])
                        N2, t_N2 = nxt(T_)
                        actf(N2[:], p1[:, 0:128], AF.Identity, [t_p1], [t_N2])
                        yield 0
                        if lvl < 5:
                            Nt2, t_Nt2 = nxt(T_)
                            S.dve(lambda e, Nt2=Nt2, p2=p2: e.tensor_copy(out=Nt2[:], in_=p2[:, 0:128]), [t_p2], [t_Nt2])
                        p3, t_p3 = nbank()
                        mm(p3[:, 0:128], N2[:], Xt[:], [t_N2, t_Xt], [t_p3])
                        yield 0
                        X2, t_X2 = nxt(T_)
                        tt("dve", X2[:], p3[:, 0:128], Xt[:], ALU.add, [t_p3, t_Xt], [t_X2])
                        Nm, t_N = N2, t_N2
                        if lvl < 5:
                            Nt, t_Nt = Nt2, t_Nt2
                        Xt, t_Xt = X2, t_X2
                        yield 0
                    pw, t_pw = nbank()
                    mm(pw[:, 0:128], kbg[:], Xt[:], [t_kbg, t_Xt], [t_pw])
                    pu, t_pu = nbank()
                    mm(pu[:, 0:128], Xt[:], vb[:], [t_Xt, t_vb], [t_pu])
                    wT, t_wT = nxt(M_)
                    actf(wT[:], pw[:, 0:128], AF.Identity, [t_pw], [t_wT])
                    u, t_u = nxt(M_)
                    S.dve(lambda e: e.tensor_copy(out=u[:], in_=pu[:, 0:128]), [t_pu], [t_u])
                    yield "rec"
                    pp1, t_pp1 = nbank()
                    mm(pp1[:, 0:128], wT[:], Sst[:], [t_wT, t_S], [t_pp1])
                    vn, t_vn = nxt(M_)
                    tt("dve", vn[:], u[:], pp1[:, 0:128], ALU.subtract, [t_u, t_pp1], [t_vn])
                    if full:
                        po, t_po = nbank()
                        mm(po[:, 0:128], Sst[:], qg[:], [t_S, t_qg], [t_po], start=True, stop=False)
                        mm(po[:, 0:128], vn[:], qk[:], [t_vn, t_qk], [t_po], start=False, stop=True)
                        osl = oT[:, 128 * ti:128 * ti + 128]
                        tt("dve", osl, po[:, 0:128], osl, ALU.add, [t_po, t_oT[ti]], [t_oT[ti]])
                    ps_, t_ps = nbank()
                    mm(ps_[:, 0:128], kd[:], vn[:], [t_kd, t_vn], [t_ps])
                    S.dve(lambda e: e.scalar_tensor_tensor(out=Sst[:], in0=Sst[:], scalar=sc("EGEND"), in1=ps_[:, 0:128],
                                                           op0=ALU.mult, op1=ALU.add), [t_S, t_SC, t_ps], [t_S])
                    yield "done"

                def run_chains(h):
                    chains = {0: [(32, False), (33, False)] + [(ti, True) for ti in range(16)],
                              1: [(33, False), (32, False)] + [(ti, False) for ti in range(31, 15, -1)] + [(ti, True) for ti in range(15, -1, -1)]}
                    pos = {0: 0, 1: 0}
                    fin = {0: 0, 1: 0}
                    active = []
                    free = list(range(WU))
                    while fin[0] < len(chains[0]) or fin[1] < len(chains[1]):
                        while free:
                            cands = [c for c in (1, 0) if pos[c] < len(chains[c])]
                            if not cands:
                                break
                            c = max(cands, key=lambda c_: len(chains[c_]) - pos[c_])
                            ti, full = chains[c][pos[c]]
                            sl = free.pop()
                            active.append([unit(h, c, ti, full, sl), c, pos[c], sl, False])
                            pos[c] += 1
                        for ent in list(active):
                            g, c, idx, sl, at_rec = ent
                            if at_rec and idx != fin[c]:
                                continue
                            r = next(g)
                            if r == "rec":
                                ent[4] = True
                            elif r == "done":
                                active.remove(ent)
                                free.append(sl)
                                fin[c] += 1

                NH_ = int(os.environ.get("K_NH", 8))
                DSTOP = int(os.environ.get("K_DSTOP", 9))
                for h in range(NH_ if DSTOP > 0 else 0):
                    for ci, off in enumerate((OFF_K, OFF_V, OFF_Q, OFF_Z)):
                        load_cast(stgD, wh[:, :, ci, :], t_wh, winsrc[:, :, off + 128 * h:off + 128 * h + 128], KC * 128)
                    project(0, raw, t_raw, seg_own + seg_oth)
                    project(0, rawc, t_rawc, seg_ctx)
                    conv_silu(raw, t_raw, kn, t_kn, SEQ, SEQ, h)
                    conv_silu(rawc, t_rawc, kn[:, SEQ:LK], t_kn, CTX, CTX, h)
                    for c0 in range(0, LK, 512):
                        l2n(kn, t_kn, c0, min(512, LK - c0), 1.0)
                    project(1, raw, t_raw, seg_own + seg_oth)
                    project(1, rawc, t_rawc, seg_ctx)
                    conv_silu(raw, t_raw, vs, t_vs, SEQ, SEQ, 8 + h)
                    conv_silu(rawc, t_rawc, vs[:, SEQ:LK], t_vs, CTX, CTX, 8 + h)
                    project(2, raw, t_raw, seg_own + seg_q2)
                    conv_silu(raw, t_raw, qn, t_qn, HALF + 1, HALF, 16 + h)
                    for c0 in range(0, HALF, 512):
                        l2n(qn, t_qn, c0, 512, 128.0 ** -0.5)
                    for d in range(2):
                        S.dve(lambda e, d=d: e.memset(Sd[d][0][:], 0.0), [], [Sd[d][1]])
                    S.dve(lambda e: e.memset(oT, 0.0), [], t_oT)
                    run_chains(h)
                    for b in range(4 if not int(os.environ.get("K_NOGATE", 0)) else 0):
                        osl = oT[:, 512 * b:512 * b + 512]
                        t_os = t_oT[4 * b:4 * b + 4]
                        sq, t_sq = nxt("blk")
                        actf(sq[:], osl, AF.Square, t_os, [t_sq])
                        pb, t_pb = nbank()
                        mm(pb, ones[:], sq[:], [t_ones, t_sq], [t_pb])
                        r1, t_r1 = nxt("blk")
                        S.dve(lambda e, r1=r1, pb=pb: e.tensor_scalar(out=r1[:], in0=pb, scalar1=1.0 / 128, scalar2=EPS,
                                                                    op0=ALU.mult, op1=ALU.add), [t_pb], [t_r1])
                        actf(r1[:], r1[:], AF.Ln, [t_r1], [t_r1])
                        actf(r1[:], r1[:], AF.Exp, [t_r1], [t_r1], scale=-0.5)
                        S.dve(lambda e, r1=r1, osl=osl: e.scalar_tensor_tensor(out=r1[:], in0=osl, scalar=ongT[:, 0:1], in1=r1[:],
                                                                             op0=ALU.mult, op1=ALU.mult),
                              list(t_os) + [t_r1, t_cst], [t_r1])
                        pz, t_pz = nbank()
                        for kc in range(KC):
                            mm(pz, wh[:, kc, 3, :], hT[:, kc, 512 * b:512 * b + 512], [t_wh] + t_hT[4 * b:4 * b + 4], [t_pz],
                               start=(kc == 0), stop=(kc == KC - 1))
                        zs, t_zs = nxt("blk")
                        actf(zs[:], pz, AF.Silu, [t_pz], [t_zs])
                        tt("pool", ogs[:, 512 * b:512 * b + 512], r1[:], zs[:], ALU.mult, [t_r1, t_zs], [t_ogs])
                    if not int(os.environ.get("K_NOOGD", 0)):
                        S.dma("sp", lambda e, h=h: e.dma_start(out=OGD[:, h, :], in_=ogs[:]), reads=[t_ogs])
                if dbg and not int(os.environ.get("K_NODBG", 0)):
                    S.dma("sp", lambda e: e.dma_start(out=dbgo["kn"], in_=kn[:]), reads=[t_kn])
                    S.dma("sp", lambda e: e.dma_start(out=dbgo["vs"], in_=vs[:]), reads=[t_vs])
                    S.dma("sp", lambda e: e.dma_start(out=dbgo["qn"], in_=qn), reads=[t_qn])
                    S.dma("sp", lambda e: e.dma_start(out=dbgo["oT"], in_=oT), reads=t_oT)
                    S.dma("sp", lambda e: e.dma_start(out=dbgo["beta"], in_=SC["BETA"][:]), reads=[t_SC])
                    S.dma("sp", lambda e: e.dma_start(out=dbgo["gc"], in_=SC["GC"][:]), reads=[t_SC])
                S.barrier()

        if stage == "full":
            YAD = nc.dram_tensor("yad_scratch", [128, 8, HALF], BF16).ap()
            winsrc = I["w_in"].rearrange("(kc p) n -> p kc n", p=128)
            _pb2 = [0]

            def nbank2():
                i = _pb2[0] % 8
                _pb2[0] += 1
                return bank(i), t_PB[i]

            def mm2(o, lhsT, rhs, r, w, start=True, stop=True):
                S.pe(lambda e: e.matmul(o, lhsT, rhs, start=start, stop=stop), reads=r, writes=w)

            def merge_pass(SRC, wproj, gate_off, init_x):
                with ExitStack() as ph:
                    def sbp(name, shape, dt=F32):
                        return ph.enter_context(nc.sbuf_tensor(un("s_" + name), list(shape), dt))
                    stg = [(sbp("stgP%d" % i, [128, 2048]), T("stgP%d" % i)) for i in range(2)]
                    Wp = sbp("Wp", [128, KC, D], BF16); t_Wp = T("Wp")
                    Wg = sbp("Wg", [128, KC, D], BF16); t_Wg = T("Wg")
                    Wo = sbp("Wo", [128, KC, D], BF16); t_Wo = T("Wo")
                    psrc = I[wproj].rearrange("(kc p) n -> p kc n", p=128)
                    osrc = I["w_out"].rearrange("(kc p) n -> p kc n", p=128)
                    for c in range(4):
                        load_cast(stg, Wp[:, 2 * c:2 * c + 2, :], t_Wp, psrc[:, 2 * c:2 * c + 2, :], 2048)
                        load_cast(stg, Wg[:, 2 * c:2 * c + 2, :], t_Wg, winsrc[:, 2 * c:2 * c + 2, gate_off:gate_off + D], 2048)
                        (sa, t_sa) = stg[c % 2]
                        sv = sa[:, 0:2048].rearrange("p (a b) -> p a b", b=D)
                        S.dma("sp", lambda e, sv=sv, c=c: e.dma_start(out=sv, in_=osrc[:, 2 * c:2 * c + 2, :]), writes=[t_sa])
                        S.pool(lambda e, sv=sv, c=c: e.tensor_tensor(
                            out=Wo[:, 2 * c:2 * c + 2, :], in0=sv, in1=gtR[:, 0:1, :].broadcast_to([128, 2, D]), op=ALU.mult),
                            [t_sa, t_gtR], [t_Wo])
                    if init_x:
                        for i in range(NT):
                            S.dma("sp", lambda e, i=i: e.dma_start(out=X1[:, i, :], in_=I["xo"][128 * i:128 * i + 128, :]),
                                  writes=[t_x1[i]])
                    inb = [(sbp("inb%d" % i, [128, KC, 512], BF16), T("inb%d" % i)) for i in range(1)] * 2
                    mbl = [(sbp("mbl%d" % i, [128, KC, 512], BF16), T("mbl%d" % i)) for i in range(2)]
                    sg = [(sbp("sg%d" % i, [128, 512]), T("sg%d" % i)) for i in range(2)]
                    for tb in range(4):
                        (ib, t_ib) = inb[tb % 2]
                        (mb, t_mb) = mbl[tb % 2]
                        S.dma("sp", lambda e, ib=ib, tb=tb: e.dma_start(out=ib[:], in_=SRC[:, :, 512 * tb:512 * tb + 512]), writes=[t_ib])
                        for ct in range(8):
                            pa, t_pa = nbank2()
                            for kc in range(KC):
                                mm2(pa, Wp[:, kc, 128 * ct:128 * ct + 128], ib[:, kc, :], [t_Wp, t_ib], [t_pa],
                                    start=(kc == 0), stop=(kc == KC - 1))
                            pg, t_pg = nbank2()
                            for kc in range(KC):
                                mm2(pg, Wg[:, kc, 128 * ct:128 * ct + 128], hT[:, kc, 512 * tb:512 * tb + 512],
                                    [t_Wg] + t_hT[4 * tb:4 * tb + 4], [t_pg], start=(kc == 0), stop=(kc == KC - 1))
                            (sgt, t_sg) = sg[ct % 2]
                            S.act(lambda e, sgt=sgt, pg=pg: e.activation(out=sgt[:], in_=pg, func=AF.Sigmoid), [t_pg], [t_sg])
                            S.dve(lambda e, mb=mb, ct=ct, pa=pa, sgt=sgt: e.tensor_tensor(
                                out=mb[:, ct, :], in0=pa, in1=sgt[:], op=ALU.mult), [t_pa, t_sg], [t_mb])
                        for tt_ in range(4):
                            ti = 4 * tb + tt_
                            for hh in range(2):
                                pc, t_pc = nbank2()
                                for kc in range(KC):
                                    mm2(pc, mb[:, kc, 128 * tt_:128 * tt_ + 128], Wo[:, kc, 512 * hh:512 * hh + 512],
                                        [t_mb, t_Wo], [t_pc], start=(kc == 0), stop=(kc == KC - 1))
                                S.dve(lambda e, ti=ti, hh=hh, pc=pc: e.tensor_tensor(
                                    out=X1[:, ti, 512 * hh:512 * hh + 512], in0=pc, in1=X1[:, ti, 512 * hh:512 * hh + 512],
                                    op=ALU.add), [t_pc, t_x1[ti]], [t_x1[ti]])
                    S.barrier()

            merge_pass(OGD, "w_proj_b", OFF_GB, True)

            with ExitStack() as ph:
                def sbp(name, shape, dt=F32):
                    return ph.enter_context(nc.sbuf_tensor(un("s_" + name), list(shape), dt))
                Wu = sbp("Wu", [128, KC, D], BF16); t_Wu = T("Wu")
                Wv = sbp("Wv", [128, KC, D], BF16); t_Wv = T("Wv")
                with ExitStack() as ph2:
                    stg = [(ph2.enter_context(nc.sbuf_tensor(un("stgA%d" % i), [128, 2048], F32)), T("stgA%d" % i)) for i in range(2)]
                    for c in range(4):
                        load_cast(stg, Wu[:, 2 * c:2 * c + 2, :], t_Wu, winsrc[:, 2 * c:2 * c + 2, OFF_U:OFF_U + D], 2048)
                        load_cast(stg, Wv[:, 2 * c:2 * c + 2, :], t_Wv, winsrc[:, 2 * c:2 * c + 2, OFF_AV:OFF_AV + D], 2048)
                    S.barrier()
                wsf = sbp("wsf", [128, 8, 128]); wsb = sbp("wsb", [128, 8, 128], BF16); t_ws = T("ws")
                lgR = sbp("lgR", [128, D]); t_lg = T("lgR")
                lbrow = sbp("lbrow", [1, D]); bsrow = sbp("bsrow", [1, D]); rsum = sbp("rsum", [1, D]); t_rows = T("rows")
                S.dma("sp", lambda e: e.dma_start(out=wsf[:], in_=I["wsT"]), writes=[t_ws])
                S.dma("sp", lambda e: e.dma_start(out=lgR[:], in_=I["lgR"]), writes=[t_lg])
                S.dma("sp", lambda e: e.dma_start(out=lbrow[:], in_=I["lbrow"]), writes=[t_rows])
                S.dma("sp", lambda e: e.dma_start(out=bsrow[:], in_=I["bsrow"]), writes=[t_rows])
                S.dve(lambda e: e.tensor_copy(out=wsb[:], in_=wsf[:]), [t_ws], [t_ws])
                for hh in range(2):
                    pr, t_pr = nbank2()
                    mm2(pr[0:1, :], ones[:, 0:1], wsf[:, 4 * hh:4 * hh + 4, :].rearrange("p a b -> p (a b)"), [t_ones, t_ws], [t_pr])
                    S.act(lambda e, pr=pr, hh=hh: e.activation(out=rsum[0:1, 512 * hh:512 * hh + 512], in_=pr[0:1, :], func=AF.Identity),
                          [t_pr], [t_rows])
                gv = sbp("gv", [128, D]); t_gv = T("gv")
                gj = sbp("gj", [128, D], BF16); t_gj = T("gj")
                vt = sbp("vt", [128, D]); t_vt = T("vt")
                vnb = [(sbp("vnb%d" % i, [128, D], BF16), T("vnb%d" % i)) for i in range(2)]
                gu = sbp("gu", [128, D]); t_gu = T("gu")
                yab = [(sbp("yab%d" % i, [128, KC, 512], BF16), T("yab%d" % i)) for i in range(1)] * 2
                st8 = [(sbp("ast%d" % i, [128, 8]), T("ast%d" % i)) for i in range(2)]
                for ti in range(NT):
                    tb, tt_ = ti // 4, ti % 4
                    (ya, t_ya) = yab[tb % 2]
                    (s8, t_s8) = st8[ti % 2]
                    (vn_, t_vn) = vnb[ti % 2]
                    tsl = slice(128 * ti, 128 * ti + 128)
                    pvs = []
                    for hh in range(2):
                        pv, t_pv = nbank2()
                        for kc in range(KC):
                            mm2(pv, hT[:, kc, tsl], Wv[:, kc, 512 * hh:512 * hh + 512], [t_hT[ti], t_Wv], [t_pv],
                                start=(kc == 0), stop=(kc == KC - 1))
                        S.act(lambda e, pv=pv, hh=hh: e.activation(out=gv[:, 512 * hh:512 * hh + 512], in_=pv, func=AF.Gelu_apprx_tanh),
                              [t_pv], [t_gv])
                    S.act(lambda e, s8=s8: e.activation(out=gj[:], in_=gv[:], func=AF.Identity, accum_out=s8[:, 0:1]),
                          [t_gv], [t_gj, t_s8])
                    S.act(lambda e, s8=s8: e.activation(out=gj[:], in_=gv[:], func=AF.Square, accum_out=s8[:, 1:2]),
                          [t_gv], [t_gj, t_s8])
                    S.dve(lambda e, s8=s8: e.tensor_scalar(out=s8[:, 2:3], in0=s8[:, 0:1], scalar1=1.0 / D, scalar2=None, op0=ALU.mult),
                          [t_s8], [t_s8])
                    S.dve(lambda e, s8=s8: e.tensor_tensor(out=s8[:, 3:4], in0=s8[:, 2:3], in1=s8[:, 2:3], op=ALU.mult), [t_s8], [t_s8])
                    S.dve(lambda e, s8=s8: e.scalar_tensor_tensor(out=s8[:, 4:5], in0=s8[:, 1:2], scalar=1.0 / D, in1=s8[:, 3:4],
                                                                  op0=ALU.mult, op1=ALU.subtract), [t_s8], [t_s8])
                    S.dve(lambda e, s8=s8: e.tensor_scalar(out=s8[:, 4:5], in0=s8[:, 4:5], scalar1=EPS, scalar2=None, op0=ALU.add),
                          [t_s8], [t_s8])
                    S.act(lambda e, s8=s8: e.activation(out=s8[:, 5:6], in_=s8[:, 4:5], func=AF.Sqrt), [t_s8], [t_s8])
                    S.dve(lambda e, s8=s8: e.reciprocal(out=s8[:, 6:7], in_=s8[:, 5:6]), [t_s8], [t_s8])
                    S.dve(lambda e, s8=s8: e.scalar_tensor_tensor(out=s8[:, 7:8], in0=s8[:, 2:3], scalar=-1.0, in1=s8[:, 6:7],
                                                                  op0=ALU.mult, op1=ALU.mult), [t_s8], [t_s8])
                    S.act(lambda e, s8=s8: e.activation(out=vt[:], in_=gv[:], func=AF.Identity, scale=s8[:, 6:7], bias=s8[:, 7:8]),
                          [t_gv, t_s8], [t_vt])
                    S.pool(lambda e, vn_=vn_: e.tensor_tensor(out=vn_[:], in0=vt[:], in1=lgR[:], op=ALU.mult), [t_vt, t_lg], [t_vn])
                    for hh in range(2):
                        pu, t_pu = nbank2()
                        for c4 in range(4):
                            ct = 4 * hh + c4
                            for kc in range(KC):
                                mm2(pu[:, 128 * c4:128 * c4 + 128], Wu[:, kc, 128 * ct:128 * ct + 128], hT[:, kc, tsl],
                                    [t_Wu, t_hT[ti]], [t_pu], start=(kc == 0), stop=(kc == KC - 1))
                        S.act(lambda e, pu=pu, hh=hh: e.activation(out=gu[:, 512 * hh:512 * hh + 512], in_=pu, func=AF.Gelu_apprx_tanh),
                              [t_pu], [t_gu])
                    for hh in range(2):
                        pm, t_pm = nbank2()
                        for c4 in range(4):
                            g = 4 * hh + c4
                            osl = pm[:, 128 * c4:128 * c4 + 128]
                            mm2(osl, vn_[:, 128 * g:128 * g + 128], wsb[:, g, :], [t_vn, t_ws], [t_pm], start=True, stop=False)
                            mm2(osl, lbrow[0:1, 128 * g:128 * g + 128], rsum[0:1, 128 * g:128 * g + 128], [t_rows], [t_pm],
                                start=False, stop=False)
                            mm2(osl, ones[0:1, :], bsrow[0:1, 128 * g:128 * g + 128], [t_ones, t_rows], [t_pm], start=False, stop=True)
                        S.dve(lambda e, ya=ya, hh=hh, tt_=tt_, pm=pm: e.tensor_tensor(
                            out=ya[:, 4 * hh:4 * hh + 4, 128 * tt_:128 * tt_ + 128], in0=pm.rearrange("p (a b) -> p a b", b=128),
                            in1=gu[:, 512 * hh:512 * hh + 512].rearrange("p (a b) -> p a b", b=128), op=ALU.mult),
                            [t_pm, t_gu], [t_ya])
                    if tt_ == 3:
                        S.dma("sp", lambda e, ya=ya, tb=tb: e.dma_start(out=YAD[:, :, 512 * tb:512 * tb + 512], in_=ya[:]), reads=[t_ya])
                S.barrier()

            merge_pass(YAD, "w_proj_a", OFF_GA, False)

        if stage == "ffn_only":
            for i in range(NT):
                S.dma("sp", lambda e, i=i: e.dma_start(out=X1[:, i, :], in_=I["xo"][128 * i:128 * i + 128, :]),
                      writes=[t_x1[i]])

        if stage in ("full", "ffn_only"):
            with ExitStack() as ph:
                jobs = [((X1[:, i, :], t_x1[i]), hT[:, :, 128 * i:128 * i + 128], t_hT[i], 0) for i in range(NT)]
                norm_tiles(ph, jobs, a2T, t_a2T, 24, 1)
                S.barrier()
            with ExitStack() as ph:
                def sbp(name, shape, dt=F32):
                    return ph.enter_context(nc.sbuf_tensor(un("s_" + name), list(shape), dt))
                gfR = sbp("gfR", [128, D]); t_gfR = T("gfR")
                fcwT = sbp("fcwT", [128, NFF, 3]); t_fcwT = T("fcwT")
                fcbT = sbp("fcbT", [128, NFF]); t_fcbT = T("fcbT")
                S.dma("sp", lambda e: e.dma_start(out=gfR[:], in_=I["gfR"]), writes=[t_gfR])
                S.dma("sp", lambda e: e.dma_start(out=fcwT[:], in_=I["fcwT"]), writes=[t_fcwT])
                S.dma("sp", lambda e: e.dma_start(out=fcbT[:], in_=I["fcbT"]), writes=[t_fcbT])
                GS = [3, 3, 3, 3, 3, 3, 3, 1]
                G0 = [0, 3, 6, 9, 12, 15, 18, 21]
                GM = 3
                stgF = [(sbp("stgF%d" % i, [128, 2048]), T("stgF%d" % i)) for i in range(2)]
                wu = [sbp("wu%d" % i, [128, KC, 2, GM * 128], BF16) for i in range(2)]
                t_wu = [T("wu0"), T("wu1")]
                wd = [sbp("wd%d" % i, [128, GM, D], BF16) for i in range(2)]
                t_wd = [T("wd0"), T("wd1")]
                actT = [sbp("actT%d" % i, [128, GM, 512], BF16) for i in range(2)]
                t_actT = [T("actT0"), T("actT1")]
                c1 = [sbp("c1_%d" % i, [128, 512]) for i in range(2)]
                t_c1 = [T("c1_0"), T("c1_1")]
                gl = [sbp("gl_%d" % i, [128, 512]) for i in range(2)]
                t_gl = [T("gl_0"), T("gl_1")]
                bsb = [sbp("bsb_%d" % i, [128, 512]) for i in range(2)]
                t_bsb = [T("bsb_0"), T("bsb_1")]
                wusrc = I["w_up"].rearrange("(kc p) n -> p kc n", p=128)
                wdsrc = I["w_down"].rearrange("(j p) n -> p j n", p=128)
                nds = 0
                nu = 0
                NG_ = int(os.environ.get("K_NG", len(GS)))
                NTB_ = int(os.environ.get("K_NTB", 4))
                NOEW_ = int(os.environ.get("K_NOEW", 0))
                for g in range(NG_):
                    G, j0 = GS[g], G0[g]
                    wb = g % 2
                    for ab in range(2):
                        for kh in range(2):
                            load_cast(stgF, wu[wb][:, 4 * kh:4 * kh + 4, ab, 0:G * 128], t_wu[wb],
                                      wusrc[:, 4 * kh:4 * kh + 4, ab * DFF + 128 * j0:ab * DFF + 128 * (j0 + G)], 4 * G * 128)
                    for jj in range(G):
                        (sa, t_sa) = stgF[nds % 2]
                        nds += 1
                        S.dma("sp", lambda e, sa=sa, j=j0 + jj: e.dma_start(out=sa[:, 0:D], in_=wdsrc[:, j, :]), writes=[t_sa])
                        S.pool(lambda e, sa=sa, wb=wb, jj=jj: e.tensor_tensor(
                            out=wd[wb][:, jj, :], in0=sa[:, 0:D], in1=gtR[:, 1, :], op=ALU.mult),
                            reads=[t_sa, t_gtR], writes=[t_wd[wb]])
                    for tb in range(NTB_):
                        ab_ = (g * 4 + tb) % 2
                        for jj in range(G):
                            j = j0 + jj
                            u = nu % 2
                            nu += 1
                            pa, pbk = 0 + u, 2 + u
                            for kc in range(KC):
                                S.pe(lambda e, kc=kc, wb=wb, jj=jj, tb=tb, pa=pa: e.matmul(
                                    bank(pa), wu[wb][:, kc, 0, 128 * jj:128 * jj + 128], hT[:, kc, 512 * tb:512 * tb + 512],
                                    start=(kc == 0), stop=(kc == KC - 1)),
                                    reads=[t_wu[wb]] + t_hT[4 * tb:4 * tb + 4], writes=[t_PB[pa]])
                            for kc in range(KC):
                                S.pe(lambda e, kc=kc, wb=wb, jj=jj, tb=tb, pbk=pbk: e.matmul(
                                    bank(pbk), wu[wb][:, kc, 1, 128 * jj:128 * jj + 128], hT[:, kc, 512 * tb:512 * tb + 512],
                                    start=(kc == 0), stop=(kc == KC - 1)),
                                    reads=[t_wu[wb]] + t_hT[4 * tb:4 * tb + 4], writes=[t_PB[pbk]])
                            S.act(lambda e, u=u, pa=pa, j=j: e.activation(
                                out=c1[u][:], in_=bank(pa), func=AF.Identity, scale=fcwT[:, j, 1:2], bias=fcbT[:, j:j + 1]),
                                reads=[t_PB[pa], t_fcwT, t_fcbT], writes=[t_c1[u]])
                            S.act(lambda e, u=u, pbk=pbk: e.activation(out=bsb[u][:], in_=bank(pbk), func=AF.Identity),
                                  reads=[t_PB[pbk]], writes=[t_bsb[u]])
                            S.dve(lambda e, u=u, pa=pa, j=j: e.scalar_tensor_tensor(
                                out=c1[u][:].rearrange("p (r t) -> p r t", t=64)[:, :, 1:64],
                                in0=bank(pa).rearrange("p (r t) -> p r t", t=64)[:, :, 0:63], scalar=fcwT[:, j, 0:1],
                                in1=c1[u][:].rearrange("p (r t) -> p r t", t=64)[:, :, 1:64], op0=ALU.mult, op1=ALU.add),
                                reads=[t_PB[pa], t_fcwT, t_c1[u]], writes=[t_c1[u]])
                            S.dve(lambda e, u=u, pa=pa, j=j: e.scalar_tensor_tensor(
                                out=c1[u][:].rearrange("p (r t) -> p r t", t=64)[:, :, 0:63],
                                in0=bank(pa).rearrange("p (r t) -> p r t", t=64)[:, :, 1:64], scalar=fcwT[:, j, 2:3],
                                in1=c1[u][:].rearrange("p (r t) -> p r t", t=64)[:, :, 0:63], op0=ALU.mult, op1=ALU.add),
                                reads=[t_PB[pa], t_fcwT, t_c1[u]], writes=[t_c1[u]])
                            S.act(lambda e, u=u: e.activation(out=gl[u][:], in_=c1[u][:], func=AF.Gelu_apprx_tanh),
                                  reads=[t_c1[u]], writes=[t_gl[u]])
                            S.pool(lambda e, u=u, ab_=ab_, jj=jj: e.tensor_tensor(
                                out=actT[ab_][:, jj, :], in0=gl[u][:], in1=bsb[u][:], op=ALU.mult),
                                reads=[t_gl[u], t_bsb[u]], writes=[t_actT[ab_]])
                        for tt in range(4):
                            ti = 4 * tb + tt
                            for hh in range(2):
                                pc = 4 + (tt * 2 + hh) % 2
                                for jj in range(G):
                                    S.pe(lambda e, ab_=ab_, jj=jj, tt=tt, hh=hh, wb=wb, pc=pc, G=G: e.matmul(
                                        bank(pc), actT[ab_][:, jj, 128 * tt:128 * tt + 128], wd[wb][:, jj, 512 * hh:512 * hh + 512],
                                        start=(jj == 0), stop=(jj == G - 1)),
                                        reads=[t_actT[ab_], t_wd[wb]], writes=[t_PB[pc]])
                                S.dve(lambda e, ti=ti, hh=hh, pc=pc: e.tensor_tensor(
                                    out=X1[:, ti, 512 * hh:512 * hh + 512], in0=bank(pc), in1=X1[:, ti, 512 * hh:512 * hh + 512],
                                    op=ALU.add), reads=[t_PB[pc], t_x1[ti]], writes=[t_x1[ti]])
                fjunk = sbp("fjunk", [128, D], BF16); t_fjunk = T("fjunk")
                for _i in range(int(os.environ.get("K_DUMMY", 0))):
                    S.dve(lambda e: e.memset(fjunk[:, 0:64], 0.0), writes=[t_fjunk])
                fo = [sbp("fo%d" % i, [128, D]) for i in range(1)] * 2
                t_fo = [T("fo0")] * 2
                fs4 = [sbp("fs4_%d" % i, [128, 4]) for i in range(2)]
                t_fs4 = [T("fs4_0"), T("fs4_1")]
                for i in range(NT if not int(os.environ.get("K_NOFIN", 0)) else 0):
                    b = i % 2
                    s4 = fs4[b]
                    S.act(lambda e, i=i, s4=s4: e.activation(out=fjunk[:], in_=X1[:, i, :], func=AF.Square, accum_out=s4[:, 0:1]),
                          reads=[t_x1[i]], writes=[t_fjunk, t_fs4[b]])
                    S.dve(lambda e, s4=s4: e.tensor_scalar(out=s4[:, 1:2], in0=s4[:, 0:1], scalar1=1.0 / D, scalar2=EPS,
                                                           op0=ALU.mult, op1=ALU.add), reads=[t_fs4[b]], writes=[t_fs4[b]])
                    S.act(lambda e, s4=s4: e.activation(out=s4[:, 2:3], in_=s4[:, 1:2], func=AF.Sqrt),
                          reads=[t_fs4[b]], writes=[t_fs4[b]])
                    S.dve(lambda e, s4=s4: e.reciprocal(out=s4[:, 3:4], in_=s4[:, 2:3]), reads=[t_fs4[b]], writes=[t_fs4[b]])
                    S.dve(lambda e, i=i, b=b, s4=s4: e.scalar_tensor_tensor(
                        out=fo[b][:], in0=X1[:, i, :], scalar=s4[:, 3:4], in1=gfR[:], op0=ALU.mult, op1=ALU.mult),
                        reads=[t_x1[i], t_fs4[b], t_gfR], writes=[t_fo[b]])
                    S.dma("sp", lambda e, i=i, b=b: e.dma_start(out=out[128 * i:128 * i + 128, :], in_=fo[b][:]),
                          reads=[t_fo[b]])
                S.barrier()

        if dbg:
            S.dma("sp", lambda e: e.dma_start(out=dbgo["hT"][:, :, 0:HALF], in_=hT[:]), reads=t_hT)
            S.dma("sp", lambda e: e.dma_start(out=dbgo["hT"][:, :, HALF:SEQ], in_=hTt), reads=t_hT)
            S.dma("sp", lambda e: e.dma_start(out=dbgo["hTc"], in_=hTc[:]), reads=t_hTc)
            S.dma("sp", lambda e: e.dma_start(out=dbgo["modT"], in_=modT[:]), reads=[t_modT])
            S.dma("sp", lambda e: e.dma_start(out=dbgo["gtR"], in_=gtR[:]), reads=[t_gtR])

        S.finish()
        with nc.Block() as block:
            S.emit(block)
    print("ops", S.nop, "waits", S.nwait, {e: len(S.prog[e]) for e in S.ALL}, "cnt", S.cnt, "dval", S.dval)
    return nc


def fm(v, n):
    return np.ascontiguousarray(np.asarray(v, np.float32).reshape(n, 128).T)


def make_in_maps(inp):
    maps = []
    x = np.asarray(inp["x"], np.float32)
    ctx = np.asarray(inp["ctx"], np.float32)
    c = np.asarray(inp["c"], np.float32)
    c_ctx = np.asarray(inp["c_ctx"], np.float32)
    w_mod = np.ascontiguousarray(np.asarray(inp["w_mod"], np.float32)[0])
    b_mod = np.asarray(inp["b_mod"], np.float32)[0]
    w_up = np.ascontiguousarray(np.asarray(inp["w_up"], np.float32)[0])
    w_down = np.ascontiguousarray(np.asarray(inp["w_down"], np.float32)[0])
    fcw = np.asarray(inp["ffn_conv_w"], np.float32)[0]
    tap = [0, 1, 2]
    w_in0 = np.ascontiguousarray(np.asarray(inp["w_in"], np.float32)[0])
    w_in1 = w_in0.copy()
    ba = w_in0[:, OFF_BA:OFF_BA + 32]
    w_in1[:, OFF_BA:OFF_BA + 32] = np.concatenate([ba[:, 8:16], ba[:, 0:8], ba[:, 24:32], ba[:, 16:24]], axis=1)
    cq = np.asarray(inp["conv_qkv"], np.float32)[0]
    alog = np.asarray(inp["a_log"], np.float32)[0]
    dtb = np.asarray(inp["dt_bias"], np.float32)[0]
    ong = np.asarray(inp["onorm_g"], np.float32)[0]
    wpa = np.ascontiguousarray(np.asarray(inp["w_proj_a"], np.float32)[0])
    wpb = np.ascontiguousarray(np.asarray(inp["w_proj_b"], np.float32)[0])
    wout = np.ascontiguousarray(np.asarray(inp["w_out"], np.float32)[0])
    lg = np.asarray(inp["a_ln_g"], np.float32)[0]
    lb = np.asarray(inp["a_ln_b"], np.float32)[0]
    abs_ = np.asarray(inp["a_bs"], np.float32)[0]
    aws = np.asarray(inp["a_ws"], np.float32)[0]
    for core in range(8):
        b, hf = core // 2, core % 2
        if hf == 0:
            xo, xt, cx = x[b, :HALF], x[b, HALF:], ctx[b]
        else:
            xf = x[b, ::-1]
            xo, xt, cx = xf[:HALF], xf[HALF:], ctx[b, ::-1]
        cvec = np.stack([fm(c[b], KC), fm(c_ctx, KC)], axis=-1)
        m = {
            "xo": np.ascontiguousarray(xo), "xt": np.ascontiguousarray(xt), "cx": np.ascontiguousarray(cx),
            "cvec": np.ascontiguousarray(cvec), "w_mod": w_mod, "b_modT": fm(b_mod, 48),
            "b_modR": np.ascontiguousarray(b_mod[None, :]),
            "g1T": fm(inp["norm1_g"][0], KC), "g2T": fm(inp["norm2_g"][0], KC),
            "gfR": np.ascontiguousarray(np.broadcast_to(np.asarray(inp["final_g"], np.float32)[None, :], (128, D))),
            "w_up": w_up, "w_down": w_down,
            "w_in": w_in0 if hf == 0 else w_in1,
            "w_proj_a": wpa, "w_proj_b": wpb, "w_out": wout,
            "lgR": np.ascontiguousarray(np.broadcast_to(lg[None, :], (128, D))),
            "lbrow": np.ascontiguousarray(lb[None, :]),
            "bsrow": np.ascontiguousarray((abs_ if hf == 0 else abs_[:, ::-1]).reshape(1, D)),
            "wsT": np.ascontiguousarray((aws if hf == 0 else aws[:, ::-1, ::-1]).transpose(2, 0, 1)),
            "cwT": np.ascontiguousarray(np.stack([fm(cq[t], 24) for t in (tap if hf == 0 else tap[::-1])], axis=-1)),
            "alogR": np.ascontiguousarray(np.broadcast_to((alog if hf == 0 else alog[::-1]).reshape(1, 16), (128, 16))),
            "dtbR": np.ascontiguousarray(np.broadcast_to((dtb if hf == 0 else dtb[::-1]).reshape(1, 16), (128, 16))),
            "ongT": np.ascontiguousarray(ong.reshape(128, 1)),
            "fcwT": np.ascontiguousarray(np.stack([fm(fcw[t], NFF) for t in (tap if hf == 0 else tap[::-1])], axis=-1)),
            "fcbT": fm(inp["ffn_conv_b"][0], NFF),
        }
        maps.append(m)
    return maps


_NC_CACHE = {}


def kernel(**inputs):
    if "full" not in _NC_CACHE:
        _NC_CACHE["full"] = build()
    nc = _NC_CACHE["full"]
    maps = make_in_maps(inputs)
    res = run_bass_kernel_spmd(nc, maps, core_ids=list(range(8)))
    outs = np.zeros((4, SEQ, D), np.float32)
    for core in range(8):
        b, hf = core // 2, core % 2
        o = res.results[core]["out"]
        if hf == 0:
            outs[b, :HALF] = o
        else:
            outs[b, HALF:] = o[::-1]
    return outs
```
